# Optimizing a Trainium2 kernel written in Bass

```python
import jax, jax.numpy as jnp
from jax import lax
import numpy as np

D_MODEL = 1024
BATCH = 4
SEQ = 8192
DEPTH = 1

GLA_HEADS = 4
GLA_DK = 64
GLA_DV = 128
GLA_GATE_RANK = 16
GLA_GATE_TAU = 16.0
GLA_CHUNK = 64
GLA_QK = GLA_HEADS * GLA_DK
GLA_V = GLA_HEADS * GLA_DV
ATTN_HEADS = 8
ATTN_HEAD_DIM = 64
ATTN_DIM = ATTN_HEADS * ATTN_HEAD_DIM
DILATED_PAIRS = ((128, 1), (512, 4), (2048, 16))
ATTN_BLOCK = 128
MIX_WIDTH = GLA_V + ATTN_DIM
IN_SPLITS = (GLA_QK, GLA_QK, GLA_V, GLA_V, GLA_GATE_RANK, ATTN_DIM, ATTN_DIM, ATTN_DIM)
IN_WIDTH = sum(IN_SPLITS)
D_FF = 2816
CONV_WIDTH = 3
N_MOD = 6
EPS = 1e-6

kernel_name = "hybrid_gla_dilated_attn_convffn_block"


def rms_norm(x, g):
    xf = x.astype(jnp.float32)
    y = xf * lax.rsqrt(jnp.mean(xf * xf, axis=-1, keepdims=True) + EPS)
    return (y * g.astype(jnp.float32)).astype(x.dtype)


def alibi_slopes(n):
    return jnp.asarray([2.0 ** (-8.0 * (h + 1) / n) for h in range(n)], dtype=jnp.float32)


def gla_group(q, k, v, log_a, r, norm_g):
    B, S, H, _ = q.shape
    n = S // GLA_CHUNK

    def chunks(t):
        return t.reshape(B, n, GLA_CHUNK, H, t.shape[-1]).transpose(1, 0, 3, 2, 4).astype(jnp.float32)

    qc = chunks(q) * (GLA_DK ** -0.5)
    kc, vc, gc = chunks(k), chunks(v), chunks(log_a)
    causal = jnp.tril(jnp.ones((GLA_CHUNK, GLA_CHUNK), dtype=bool))

    def step(state, inp):
        qi, ki, vi, gi = inp
        b = jnp.cumsum(gi, axis=-2)
        b_last = b[..., -1, :]
        o_inter = jnp.einsum('bhck,bhkv->bhcv', qi * jnp.exp(b), state)
        diff = b[:, :, :, None, :] - b[:, :, None, :, :]
        decay = jnp.exp(jnp.where(causal[:, :, None], diff, -jnp.inf))
        scores = jnp.einsum('bhik,bhjk,bhijk->bhij', qi, ki, decay)
        o = o_inter + jnp.einsum('bhij,bhjv->bhiv', scores, vi)
        state = state * jnp.exp(b_last)[..., None] + jnp.einsum(
            'bhck,bhcv->bhkv', ki * jnp.exp(b_last[:, :, None, :] - b), vi)
        return state, o

    state0 = jnp.zeros((B, H, GLA_DK, GLA_DV), jnp.float32)
    _, o = lax.scan(step, state0, (qc, kc, vc, gc))
    o = o.transpose(1, 0, 3, 2, 4).reshape(B, S, H, GLA_DV)
    o = rms_norm(o, norm_g).reshape(B, S, H * GLA_DV)
    return (o * jax.nn.silu(r.astype(jnp.float32))).astype(r.dtype)


def dilated_branch(q, k, v, slopes, window, dilation):
    B, S, H, E = q.shape
    L = S // dilation
    span = window // dilation
    nb = -(-L // ATTN_BLOCK)
    pad = nb * ATTN_BLOCK - L

    def to_blocks(t):
        t = t.reshape(B, L, dilation, H, E).transpose(0, 2, 3, 1, 4)
        t = jnp.pad(t, ((0, 0), (0, 0), (0, 0), (0, pad), (0, 0)))
        return t.reshape(B, dilation, H, nb, ATTN_BLOCK, E)

    def with_prev(t):
        prev = jnp.pad(t, ((0, 0), (0, 0), (0, 0), (1, 0), (0, 0), (0, 0)))[:, :, :, :-1]
        return jnp.concatenate([prev, t], axis=4)

    qb = to_blocks(q)
    kw = with_prev(to_blocks(k))
    vw = with_prev(to_blocks(v))
    s = jnp.einsum('bdhnqe,bdhnke->bdhnqk', qb, kw).astype(jnp.float32) * (E ** -0.5)
    iq = jnp.arange(ATTN_BLOCK)[:, None]
    ik = jnp.arange(2 * ATTN_BLOCK)[None, :]
    rel = iq + ATTN_BLOCK - ik
    key_idx = jnp.arange(nb)[:, None, None] * ATTN_BLOCK + ik - ATTN_BLOCK
    valid = (rel >= 0) & (rel <= span) & (key_idx >= 0)
    alibi = -slopes[:, None, None, None] * (dilation * rel).astype(jnp.float32)
    s = jnp.where(valid, s + alibi, -jnp.inf)
    m = jnp.max(s, axis=-1, keepdims=True)
    p = jnp.exp(s - m)
    den = jnp.sum(p, axis=-1, keepdims=True)
    o = jnp.einsum('bdhnqk,bdhnke->bdhnqe', p, vw.astype(jnp.float32)) / den
    lse = (m + jnp.log(den))[..., 0]
    o = o.reshape(B, dilation, H, nb * ATTN_BLOCK, E)[:, :, :, :L]
    o = o.transpose(0, 3, 1, 2, 4).reshape(B, S, H, E)
    lse = lse.reshape(B, dilation, H, nb * ATTN_BLOCK)[:, :, :, :L]
    lse = lse.transpose(0, 3, 1, 2).reshape(B, S, H)
    return o, lse


def dilated_attention_group(q, k, v):
    slopes = alibi_slopes(ATTN_HEADS)
    outs, lses = [], []
    for window, dilation in DILATED_PAIRS:
        o, lse = dilated_branch(q, k, v, slopes, window, dilation)
        outs.append(o)
        lses.append(lse)
    weights = jax.nn.softmax(jnp.stack(lses, axis=0), axis=0)
    return jnp.einsum('gbsh,gbshe->bshe', weights, jnp.stack(outs, axis=0))


def causal_depthwise_conv(u, w, b):
    K, C = w.shape
    y = lax.conv_general_dilated(u, w[:, None, :], window_strides=(1,), padding=[(K - 1, 0)],
                                 dimension_numbers=('NWC', 'WIO', 'NWC'), feature_group_count=C)
    return y + b


def setup_inputs(seed: int = 0) -> dict:
    key = jax.random.key(seed)
    ks = jax.random.split(key, 17)
    f32 = jnp.float32
    L = DEPTH

    def nrm(k, shape, scale):
        return jax.random.normal(k, shape, f32) * scale

    return {
        "x": nrm(ks[0], (BATCH, SEQ, D_MODEL), 1.0),
        "c": nrm(ks[1], (BATCH, D_MODEL), 1.0),
        "w_ada": nrm(ks[2], (L, D_MODEL, N_MOD * D_MODEL), 0.5 * D_MODEL ** -0.5),
        "b_ada": nrm(ks[3], (L, N_MOD * D_MODEL), 0.02),
        "norm1_g": 1.0 + nrm(ks[4], (L, D_MODEL), 0.02),
        "w_in": nrm(ks[5], (L, D_MODEL, IN_WIDTH), D_MODEL ** -0.5),
        "gla_w_gate": nrm(ks[6], (L, GLA_GATE_RANK, GLA_QK), GLA_GATE_RANK ** -0.5),
        "gla_b_gate": nrm(ks[7], (L, GLA_QK), 0.02),
        "gla_norm_g": 1.0 + nrm(ks[8], (L, GLA_DV), 0.02),
        "q_norm_g": 1.0 + nrm(ks[9], (L, ATTN_HEAD_DIM), 0.02),
        "k_norm_g": 1.0 + nrm(ks[10], (L, ATTN_HEAD_DIM), 0.02),
        "w_out": nrm(ks[11], (L, MIX_WIDTH, D_MODEL), MIX_WIDTH ** -0.5),
        "norm2_g": 1.0 + nrm(ks[12], (L, D_MODEL), 0.02),
        "w_up": nrm(ks[13], (L, D_MODEL, 2 * D_FF), D_MODEL ** -0.5),
        "conv_w": nrm(ks[14], (L, CONV_WIDTH, 2 * D_FF), CONV_WIDTH ** -0.5),
        "conv_b": nrm(ks[15], (L, 2 * D_FF), 0.02),
        "w_down": nrm(ks[16], (L, D_FF, D_MODEL), D_FF ** -0.5),
    }


def reference(x, c, w_ada, b_ada, norm1_g, w_in, gla_w_gate, gla_b_gate, gla_norm_g,
              q_norm_g, k_norm_g, w_out, norm2_g, w_up, conv_w, conv_b, w_down):
    B, S, _ = x.shape
    cond = jax.nn.silu(c)
    split_at = np.cumsum(IN_SPLITS)[:-1].tolist()
    for l in range(DEPTH):
        mod = (cond @ w_ada[l] + b_ada[l])[:, None, :]
        sh1, sc1, g1, sh2, sc2, g2 = jnp.split(mod, N_MOD, axis=-1)

        h = rms_norm(x, norm1_g[l]) * (1 + sc1) + sh1
        proj = h @ w_in[l]
        gq, gk, gv, gr, glr, aq, ak, av = jnp.split(proj, split_at, axis=-1)
        log_a = jax.nn.log_sigmoid((glr @ gla_w_gate[l] + gla_b_gate[l]).astype(jnp.float32)) / GLA_GATE_TAU
        y_gla = gla_group(gq.reshape(B, S, GLA_HEADS, GLA_DK), gk.reshape(B, S, GLA_HEADS, GLA_DK),
                          gv.reshape(B, S, GLA_HEADS, GLA_DV), log_a.reshape(B, S, GLA_HEADS, GLA_DK),
                          gr, gla_norm_g[l])
        qa = rms_norm(aq.reshape(B, S, ATTN_HEADS, ATTN_HEAD_DIM), q_norm_g[l])
        ka = rms_norm(ak.reshape(B, S, ATTN_HEADS, ATTN_HEAD_DIM), k_norm_g[l])
        y_att = dilated_attention_group(qa, ka, av.reshape(B, S, ATTN_HEADS, ATTN_HEAD_DIM))
        mixed = jnp.concatenate([y_gla.astype(x.dtype), y_att.reshape(B, S, ATTN_DIM).astype(x.dtype)], axis=-1)
        x = x + g1 * (mixed @ w_out[l])

        h = rms_norm(x, norm2_g[l]) * (1 + sc2) + sh2
        u = causal_depthwise_conv(h @ w_up[l], conv_w[l], conv_b[l])
        u_gate, u_val = jnp.split(u, 2, axis=-1)
        x = x + g2 * ((jax.nn.silu(u_gate) * u_val) @ w_down[l])
    return x
```

```python
import os
from contextlib import ExitStack
import numpy as np
import ml_dtypes
import concourse.bass as bass
import concourse.mybir as mybir
from concourse.bass_utils import run_bass_kernel_spmd

F32 = mybir.dt.float32
BF16 = mybir.dt.bfloat16
ALU = mybir.AluOpType
AF = mybir.ActivationFunctionType
AX = mybir.AxisListType

D = 1024
NCH = 8
IN_W = 3088
DFF = 2816
NFC = 44
NGC = 22
EPS = 1e-6
C_GQ, C_GK, C_GV, C_GR, C_GLR, C_AQ, C_AK, C_AV = 0, 256, 512, 1024, 1536, 1552, 2064, 2576
DILS = (1, 4, 16)
BIGD = float(2 ** 18)
ENTRIES = [(g, m) for g in range(3) for m in range(DILS[g] + 1)]
KVRING = 18


class Buf:
    __slots__ = ("w", "r", "name")

    def __init__(self, name=""):
        self.w = None
        self.r = {}
        self.name = name


class Sched:
    def __init__(self, nc, sems, dma_sems):
        self.nc = nc
        self.engs = {}
        for n in ("pe", "act", "dve", "pool", "sp"):
            self.engs[n] = {"sem": sems[n], "cnt": 0, "waited": {}, "ops": []}
        self.dma_sems = dma_sems
        self.dma_state = [0] * len(dma_sems)
        self.dma_rr = 0
        self.n_sw = 4
        self.dma_rr_sw = 0
        self.semobj = {}
        for n in sems:
            self.semobj[id(sems[n])] = sems[n]
        for s in dma_sems:
            self.semobj[id(s)] = s

    def _deps(self, eng, reads, writes):
        deps = {}

        def add(tok):
            if tok is None:
                return
            s, v = tok
            if deps.get(s, 0) < v:
                deps[s] = v
        for b in reads:
            add(b.w)
        for b in writes:
            add(b.w)
            for s, v in b.r.items():
                add((s, v))
        e = self.engs[eng]
        waits = []
        for s, v in deps.items():
            if eng == "pe" and s == id(e["sem"]):
                continue
            if e["waited"].get(s, 0) < v:
                e["waited"][s] = v
                waits.append((self.semobj[s], v))
        return waits

    def _commit(self, tok, reads, writes):
        s, v = tok
        for b in writes:
            b.w = tok
            b.r = {}
        for b in reads:
            if b.r.get(s, 0) < v:
                b.r[s] = v

    def op(self, eng, fn, reads=(), writes=()):
        e = self.engs[eng]
        waits = self._deps(eng, reads, writes)
        e["cnt"] += 1
        tok = (id(e["sem"]), e["cnt"])
        e["ops"].append((waits, fn, e["sem"], 1))
        self._commit(tok, reads, writes)

    def dma(self, eng, fn, reads=(), writes=()):
        e = self.engs[eng]
        if eng == "pool":
            i = self.dma_rr_sw
            self.dma_rr_sw = (self.dma_rr_sw + 1) % self.n_sw
        else:
            i = self.n_sw + self.dma_rr
            self.dma_rr = (self.dma_rr + 1) % (len(self.dma_sems) - self.n_sw)
        sem = self.dma_sems[i]
        waits = self._deps(eng, reads, writes)
        prev = self.dma_state[i]
        if prev > 0 and e["waited"].get(id(sem), 0) < prev:
            e["waited"][id(sem)] = prev
            waits.append((sem, prev))
        self.dma_state[i] = prev + 16
        tok = (id(sem), prev + 16)
        e["ops"].append((waits, fn, sem, 16))
        self._commit(tok, reads, writes)

    def barrier(self):
        toks = []
        for n, e in self.engs.items():
            if e["cnt"] > 0:
                toks.append((e["sem"], e["cnt"]))
        for i, s in enumerate(self.dma_sems):
            if self.dma_state[i] > 0:
                toks.append((s, self.dma_state[i]))
        for n, e in self.engs.items():
            waits = []
            for s, v in toks:
                if s is e["sem"]:
                    continue
                if e["waited"].get(id(s), 0) < v:
                    e["waited"][id(s)] = v
                    waits.append((s, v))
            if waits:
                e["ops"].append((waits, None, None, 0))

    def emit(self, eng, h):
        for waits, fn, sem, inc in self.engs[eng]["ops"]:
            for s, v in waits:
                h.wait_ge(s, v)
            if fn is not None:
                fn(h).then_inc(sem, inc)


def build(NM, debug=False):
    STOP = int(os.environ.get('KSTOP', '2'))
    NOW = int(os.environ.get('KNOW', '1'))
    CUT = int(os.environ.get('KCUT', '9'))
    PXN = int(os.environ.get('PXN', '2')); PHT = int(os.environ.get('PHT', '2')); PQT = int(os.environ.get('PQT', '2'))
    NL = 2 * NM
    HALO = NM - 1
    KV0 = max(0, HALO - 16)
    NQ = NM + 1
    nc = bass.Bass("TRN2", target_bir_lowering=False)

    def din(name, shape, dt=F32):
        return nc.dram_tensor(name, list(shape), dt, kind="ExternalInput").ap()

    xl = din("xl", [NL * 128, D])
    cT = din("cT", [128, 8])
    w_ada = din("w_ada", [D, 6 * D])
    b_ada = din("b_ada", [1, 6 * D])
    g1row = din("g1row", [1, D])
    g2row = din("g2row", [1, D])
    w_in = din("w_in", [D, IN_W])
    wg = din("wg", [16, 256])
    nbgT = din("nbgT", [128, 2])
    ggb = din("ggb", [128, 512])
    gqc = din("gqc", [128, 1])
    gkc = din("gkc", [128, 1])
    w_out = din("w_out", [D, D])
    w_up = din("w_up", [D, 2 * DFF])
    cwT = din("cwT", [128, 3, NFC])
    cbT = din("cbT", [128, NFC])
    w_down = din("w_down", [DFF, D])
    flag = din("flag", [128, 1])
    ident = din("ident", [128, 128], BF16)
    nsI = din("nsI", [128, 8, 128], BF16)
    dn = din("dn", [128, 24, 128], BF16)
    tri4 = din("tri4", [128, 4, 128])
    out = nc.dram_tensor("out", [NM * 128, D], F32, kind="ExternalOutput").ap()
    x1d = nc.dram_tensor("x1d", [NQ * 128, D], F32,
                         kind="ExternalOutput" if debug else "Internal").ap()

    es = ExitStack()
    with es:
        def sb(name, shape, dt=F32):
            return es.enter_context(nc.sbuf_tensor(name, list(shape), dt))

        def ps(name, shape, dt=F32):
            return es.enter_context(nc.psum_tensor(name, list(shape), dt))

        sems = {n: es.enter_context(nc.semaphore("s_" + n)) for n in ("pe", "act", "dve", "pool", "sp")}
        dma_sems = [es.enter_context(nc.semaphore("s_dma%d" % i)) for i in range(24)]
        K = Sched(nc, sems, dma_sems)

        pT = ps("pT", [128, 1024], BF16); bT = Buf("T")
        pP = [ps("pP%d" % i, [128, 512]) for i in range(2)]; bP = [Buf("P0"), Buf("P1")]
        pF = ps("pF", [128, 4, 128]); bF = Buf("F")
        pG = ps("pG", [128, 512]); bG = Buf("G")
        pS = [ps("pS%d" % i, [128, 512]) for i in range(2)]; bS = [Buf("S0"), Buf("S1")]
        pO = ps("pO", [128, 4, 128]); bO = Buf("O")

        ident_sb = sb("ident_sb", [128, 128], BF16); b_ident = Buf()
        flag_sb = sb("flag_sb", [128, 1]); b_flag = Buf()
        modT = sb("modT", [128, 48]); b_modT = Buf()
        G1 = sb("G1c", [128, 8]); G2 = sb("G2c", [128, 8]); b_G = Buf()
        g1b = sb("g1b", [128, D]); g2b = sb("g2b", [128, D]); b_gb = Buf()

        def dma_in(eng, dst, src, wbuf):
            K.dma(eng, lambda h: h.dma_start(out=dst, in_=src), writes=[wbuf])

        epsc = sb("epsc", [128, 1])
        b_eps = Buf()
        K.op("pool", lambda h: h.memset(epsc[:], EPS), writes=[b_eps])
        dma_in("sp", ident_sb[:], ident, b_ident)
        dma_in("sp", flag_sb[:], flag, b_flag)

        with ExitStack() as e0:
            def sb0(name, shape, dt=F32):
                return e0.enter_context(nc.sbuf_tensor(name, list(shape), dt))
            c_sb = sb0("c_sb", [128, 8]); b_c = Buf()
            cond = sb0("cond", [128, 8]); b_cond = Buf()
            wa = [sb0("wa%d" % i, [128, 8, 512]) for i in range(3)]; b_wa = [Buf() for _ in range(3)]
            brow = sb0("brow", [1, 6 * D]); b_brow = Buf()
            mrow = sb0("mrow", [1, 6 * D]); b_mrow = Buf()
            grow = sb0("grow", [1, 2 * D]); b_grow = Buf()
            ones1 = sb0("ones1", [1, 128]); b_ones = Buf()
            dma_in("sp", c_sb[:], cT, b_c)
            dma_in("sp", brow[:], b_ada, b_brow)
            dma_in("sp", grow[:, 0:D], g1row, b_grow)
            dma_in("sp", grow[:, D:2 * D], g2row, b_grow)
            K.op("dve", lambda h: h.memset(ones1[:], 1.0), writes=[b_ones])
            K.op("act", lambda h: h.activation(out=cond[:], in_=c_sb[:], func=AF.Silu),
                 reads=[b_c], writes=[b_cond])
            w_ada_v = w_ada.rearrange("(p k) n -> p k n", k=8)
            for blk in range(12):
                s = blk % 3
                dma_in("sp", wa[s][:], w_ada_v[:, :, blk * 512:(blk + 1) * 512], b_wa[s])
                pb = pP[blk % 2]; bb = bP[blk % 2]
                for k in range(8):
                    K.op("pe", lambda h, pb=pb, s=s, k=k: h.matmul(
                        pb[0:1, :], lhsT=cond[:, k:k + 1], rhs=wa[s][:, k, :],
                        start=(k == 0), stop=(k == 7)),
                        reads=[b_cond, b_wa[s]], writes=[bb])
                K.op("dve", lambda h, pb=pb, blk=blk: h.tensor_tensor(
                    out=mrow[:, blk * 512:(blk + 1) * 512], in0=pb[0:1, :],
                    in1=brow[:, blk * 512:(blk + 1) * 512], op=ALU.add),
                    reads=[bb, b_brow], writes=[b_mrow])
            for j in range(48):
                K.op("pe", lambda h, j=j: h.matmul(
                    pG[:, j:j + 1], lhsT=mrow[0:1, j * 128:(j + 1) * 128], rhs=ones1[0:1, 0:1],
                    start=True, stop=True), reads=[b_mrow, b_ones], writes=[bG])
            K.op("dve", lambda h: h.tensor_copy(out=modT[:], in_=pG[:, 0:48]), reads=[bG], writes=[b_modT])
            for gi, (gb, col0) in enumerate(((g1b, 2 * D), (g2b, 5 * D))):
                for hf in range(2):
                    pb = pP[hf]; bb = bP[hf]
                    K.op("pe", lambda h, pb=pb, col0=col0, hf=hf: h.matmul(
                        pb[:, :], lhsT=ones1[0:1, :], rhs=mrow[0:1, col0 + hf * 512: col0 + (hf + 1) * 512],
                        start=True, stop=True), reads=[b_mrow, b_ones], writes=[bb])
                    K.op("dve", lambda h, pb=pb, gb=gb, hf=hf: h.tensor_copy(
                        out=gb[:, hf * 512:(hf + 1) * 512], in_=pb[:, :]), reads=[bb], writes=[b_gb])
            n1c = sb0("n1c", [128, 16]); b_n1c = Buf()
            for j in range(16):
                K.op("pe", lambda h, j=j: h.matmul(
                    pG[:, 64 + j:65 + j], lhsT=grow[0:1, j * 128:(j + 1) * 128], rhs=ones1[0:1, 0:1],
                    start=True, stop=True), reads=[b_grow, b_ones], writes=[bG])
            K.op("dve", lambda h: h.tensor_copy(out=n1c[:], in_=pG[:, 64:80]), reads=[bG], writes=[b_n1c])
            K.op("dve", lambda h: h.scalar_tensor_tensor(
                out=G1[:], in0=modT[:, 8:16], scalar=1.0, in1=n1c[:, 0:8], op0=ALU.add, op1=ALU.mult),
                reads=[b_modT, b_n1c], writes=[b_G])
            K.op("dve", lambda h: h.scalar_tensor_tensor(
                out=G2[:], in0=modT[:, 32:40], scalar=1.0, in1=n1c[:, 8:16], op0=ALU.add, op1=ALU.mult),
                reads=[b_modT, b_n1c], writes=[b_G])
            K.barrier()
        sh1 = modT[:, 0:8]
        sh2 = modT[:, 24:32]

        with ExitStack() as e1:
            def sb1(name, shape, dt=F32):
                return e1.enter_context(nc.sbuf_tensor(name, list(shape), dt))
            win = sb1("win", [128, 8, IN_W], BF16); b_wins = [Buf() for _ in range(4)]
            wout = sb1("wout", [128, 8, D], BF16); b_wout = Buf()
            wg_sb = sb1("wg_sb", [16, 256], BF16); b_wg = Buf()
            nbg = sb1("nbg", [128, 2]); gg = sb1("gg", [128, 512])
            gq_sb = sb1("gq_sb", [128, 1]); gk_sb = sb1("gk_sb", [128, 1]); b_small = Buf()
            nsI_sb = sb1("nsI_sb", [128, 8, 128], BF16); dn_sb = sb1("dn_sb", [128, 24, 128], BF16)
            tri_sb = sb1("tri_sb", [128, 4, 128]); b_cst = Buf()
            stg = [sb1("stg%d" % i, [128, 8, 128]) for i in range(2)]; b_stg = [Buf(), Buf()]

            w_in_v = w_in.rearrange("(c p) n -> p c n", p=128)
            for i in range(4):
                c0 = i * 772
                K.dma("pool", lambda h, c0=c0: h.dma_start(out=win[:, :, c0:c0 + 772], in_=w_in_v[:, :, c0:c0 + 772]),
                      writes=[b_wins[i]])
            K.dma("pool", lambda h: h.dma_start(out=wg_sb[:], in_=wg), writes=[b_wg])
            dma_in("sp", nbg[:], nbgT, b_small); dma_in("sp", gg[:], ggb, b_small)
            dma_in("sp", gq_sb[:], gqc, b_small); dma_in("sp", gk_sb[:], gkc, b_small)
            dma_in("sp", nsI_sb[:], nsI, b_cst); dma_in("sp", dn_sb[:], dn, b_cst)
            dma_in("sp", tri_sb[:], tri4, b_cst)
            w_out_v = w_out.rearrange("(c p) n -> p c n", p=128)
            for i in range(8):
                s = i % 2
                dma_in("sp", stg[s][:], w_out_v[:, :, i * 128:(i + 1) * 128], b_stg[s])
                K.op("dve", lambda h, s=s, i=i: h.tensor_tensor(
                    out=wout[:, :, i * 128:(i + 1) * 128], in0=stg[s][:],
                    in1=g1b[:, i * 128:(i + 1) * 128].rearrange("p (o n) -> p o n", o=1).broadcast_to([128, 8, 128]),
                    op=ALU.mult), reads=[b_stg[s], b_gb], writes=[b_wout])

            xt = [sb1("xt%d" % i, [128, D]) for i in range(3)]; b_xt = [Buf() for _ in range(3)]
            junk = sb1("junk", [128, D], BF16); b_junk = Buf()
            sq = sb1("sq", [128, 512]); b_sq = Buf()
            ss = sb1("ss", [128, 4]); b_ss = Buf()
            xn = [sb1("xn%d" % i, [128, D], BF16) for i in range(2)]; b_xn = [Buf(), Buf()]
            hT = [sb1("hT%d" % i, [128, 8, 128], BF16) for i in range(2)]; b_hT = [Buf(), Buf()]; b_hTo = [Buf(), Buf()]
            QT = [sb1("QT%d" % i, [128, 2, 4, 128], BF16) for i in range(2)]; b_QT = [Buf(), Buf()]
            KT = [sb1("KT%d" % i, [128, 4, 128], BF16) for i in range(KVRING)]; b_KT = [Buf() for _ in range(KVRING)]
            VA = [sb1("VA%d" % i, [128, 8, 65], BF16) for i in range(KVRING)]; b_VA = [Buf() for _ in range(KVRING)]
            Pt = [sb1("Pt%d" % i, [128, 512], BF16) for i in range(3)]; b_Pt = [Buf() for _ in range(3)]
            qkn = sb1("qkn", [128, 512], BF16); b_qkn = Buf()
            ss8 = sb1("ss8", [128, 16]); b_ss8 = Buf()
            glrT = sb1("glrT", [16, 128], BF16); b_glrT = Buf()
            ge = sb1("ge", [128, 2, 128]); b_ge = Buf()
            gl = sb1("gl", [128, 2, 128]); b_gl = Buf()
            gbl = sb1("gbl", [128, 2, 128]); b_gbl = Buf()
            nbl = sb1("nbl", [128, 2]); b_nbl = Buf()
            Eb = sb1("Eb", [128, 2, 128]); Enb = sb1("Enb", [128, 2, 128]); Ee = sb1("Ee", [128, 2, 128])
            b_Eb = Buf(); b_Enb = Buf(); b_Ee = Buf()
            zeros = sb1("zeros", [128, 128]); b_zeros = Buf()
            QdT = sb1("QdT", [128, 2, 2, 128], BF16); KdT = sb1("KdT", [128, 2, 128], BF16)
            KeT = sb1("KeT", [128, 2, 128], BF16); b_QdT = Buf(); b_KdT = Buf(); b_KeT = Buf()
            Ke = sb1("Ke", [128, 256], BF16); b_Ke = Buf()
            vsb = sb1("vsb", [128, 512], BF16); b_vsb = Buf()
            sr = sb1("sr", [128, 512]); b_sr = Buf()
            At = sb1("At", [128, 4, 128], BF16); b_At = Buf()
            S = sb1("S", [128, 2, 128]); Sbf = sb1("Sbf", [128, 2, 128], BF16); b_S = Buf(); b_Sbf = Buf()
            ss4 = sb1("ss4", [128, 8]); b_ss4 = Buf()
            rden = sb1("rden", [128, 8]); b_rden = Buf()
            mixs = [sb1("mix%d" % i, [128, D], BF16) for i in range(2)]; bmixs = [Buf(), Buf()]
            mixT = sb1("mixT", [128, 8, 128], BF16); b_mixT = Buf()

            K.op("pool", lambda h: h.memset(zeros[:], 0.0), writes=[b_zeros])
            K.op("pool", lambda h: h.memset(QdT[:], 0.0), writes=[b_QdT])
            for i in range(2):
                K.op("pool", lambda h, i=i: h.memset(QT[i][:], 0.0), writes=[b_QT[i]])
            K.op("pool", lambda h: h.memset(S[:], 0.0), writes=[b_S])
            K.op("pool", lambda h: h.memset(Sbf[:], 0.0), writes=[b_Sbf])

            def proj_tok(j, col0, pi, extra=()):
                hb = hT[j % PHT]
                for c in range(8):
                    K.op("pe", lambda h, c=c, hb=hb: h.matmul(
                        pP[pi][:, :], lhsT=hb[:, c, :], rhs=win[:, c, col0:col0 + 512],
                        start=(c == 0), stop=(c == 7)), reads=[b_hT[j % PHT], b_hTo[j % PHT]] + b_wins + list(extra), writes=[bP[pi]])

            def proj_feat(j, col0, dst, wbuf, width=128):
                hb = hT[j % PHT]
                for c in range(8):
                    K.op("pe", lambda h, c=c, hb=hb: h.matmul(
                        dst, lhsT=win[:, c, col0:col0 + width], rhs=hb[:, c, :],
                        start=(c == 0), stop=(c == 7)), reads=[b_hT[j % PHT], b_hTo[j % PHT]] + b_wins, writes=[wbuf])

            loaded_x = set()

            def load_x(j):
                if j in loaded_x or j >= NL:
                    return
                loaded_x.add(j)
                dma_in("sp", xt[j % 3][:], xl[j * 128:(j + 1) * 128, :], b_xt[j % 3])

            def front(j):
                mixb, bmix = mixs[j % 2], bmixs[j % 2]
                pGo = pP[0][:, :].rearrange("p (a b) -> p a b", a=4)
                kind = "FULL" if j >= HALO else ("GK" if j >= KV0 else "G")
                xb, bxb = xt[j % 3], b_xt[j % 3]
                xnb, bxn = xn[j % PXN], b_xn[j % PXN]
                hb, bhb = hT[j % PHT], b_hT[j % PHT]
                sl = j % KVRING
                va, bva = VA[sl], b_VA[sl]
                qb = QT[j % PQT]
                load_x(j)
                if kind != "FULL":
                    load_x(j + 1)
                    load_x(j + 2)

                def o_norm():
                    K.op("act", lambda h: h.activation(out=junk[:], in_=xb[:], func=AF.Square, accum_out=ss[:, 0:1]),
                         reads=[bxb], writes=[b_junk, b_ss])
                    K.op("act", lambda h: h.activation(out=ss[:, 2:3], in_=ss[:, 0:1], func=AF.Ln, scale=1.0 / D, bias=epsc[:, 0:1]),
                         reads=[b_ss], writes=[b_ss])
                    K.op("act", lambda h: h.activation(out=ss[:, 1:2], in_=ss[:, 2:3], func=AF.Exp, scale=-0.5),
                         reads=[b_ss], writes=[b_ss])

                def o_xn():
                    K.op("dve", lambda h: h.tensor_scalar(out=xnb[:], in0=xb[:], scalar1=ss[:, 1:2], scalar2=None, op0=ALU.mult),
                         reads=[bxb, b_ss], writes=[bxn])

                def o_xT():
                    for c in range(8):
                        K.op("pe", lambda h, c=c: h.transpose(pT[:, c * 128:(c + 1) * 128], xnb[:, c * 128:(c + 1) * 128], ident_sb[:]),
                             reads=[bxn, b_ident], writes=[bT])

                def o_hT():
                    for c in range(8):
                        K.op("dve", lambda h, c=c: h.tensor_scalar(
                            out=hb[:, c, :], in0=pT[:, c * 128:(c + 1) * 128], scalar1=G1[:, c:c + 1], scalar2=sh1[:, c:c + 1],
                            op0=ALU.mult, op1=ALU.add), reads=[bT, b_G, b_modT], writes=[bhb])

                def o_glr():
                    proj_feat(j, C_GLR, pG[0:16, 0:128], bG, width=16)

                def o_gq():
                    for q in range(2):
                        proj_feat(j, C_GQ + q * 128, pF[:, q, :], bF)

                def o_gk():
                    for q in range(2):
                        proj_feat(j, C_GK + q * 128, pF[:, 2 + q, :], bF)

                def o_glrT():
                    K.op("dve", lambda h: h.tensor_copy(out=glrT[:], in_=pG[0:16, 0:128]), reads=[bG], writes=[b_glrT])

                def o_z():
                    for f in range(2):
                        K.op("pe", lambda h, f=f: h.matmul(pG[:, 128 + f * 128:256 + f * 128], lhsT=wg_sb[0:16, f * 128:(f + 1) * 128],
                                                           rhs=glrT[0:16, :], start=True, stop=True),
                             reads=[b_glrT, b_wg], writes=[bG])

                def o_ge():
                    for f in range(2):
                        K.op("act", lambda h, f=f: h.activation(out=ge[:, f, :], in_=pG[:, 128 + f * 128:256 + f * 128], func=AF.Exp,
                                                                scale=-1.0, bias=nbg[:, f:f + 1]),
                             reads=[bG, b_small], writes=[b_ge])
                    K.op("act", lambda h: h.activation(out=gl[:], in_=ge[:], func=AF.Ln, bias=1.0, scale=1.0),
                         reads=[b_ge], writes=[b_gl])

                def o_scan():
                    for f in range(2):
                        K.op("dve", lambda h, f=f: h.tensor_tensor_scan(out=gbl[:, f, :], data0=zeros[:], data1=gl[:, f, :],
                                                                        initial=0.0, op0=ALU.add, op1=ALU.add),
                             reads=[b_gl, b_zeros], writes=[b_gbl])
                    K.op("dve", lambda h: h.tensor_scalar(out=nbl[:], in0=gbl[:, :, 127], scalar1=-1.0 / 16.0, scalar2=None,
                                                          op0=ALU.mult), reads=[b_gbl], writes=[b_nbl])

                def o_EbEe():
                    K.op("act", lambda h: h.activation(out=Eb[:], in_=gbl[:], func=AF.Exp, scale=-1.0 / 16.0),
                         reads=[b_gbl], writes=[b_Eb])
                    for f in range(2):
                        K.op("act", lambda h, f=f: h.activation(out=Ee[:, f, :], in_=gbl[:, f, :], func=AF.Exp, scale=1.0 / 16.0,
                                                                bias=nbl[:, f:f + 1]), reads=[b_gbl, b_nbl], writes=[b_Ee])

                def o_Enb():
                    K.op("act", lambda h: h.activation(out=Enb[:], in_=gbl[:], func=AF.Exp, scale=1.0 / 16.0),
                         reads=[b_gbl], writes=[b_Enb])

                def o_KeT():
                    K.op("dve", lambda h: h.tensor_tensor(out=KeT[:], in0=pF[:, 2:4, :], in1=Ee[:], op=ALU.mult),
                         reads=[bF, b_Ee], writes=[b_KeT])

                def o_QdKd():
                    for pos in range(2):
                        p0 = pos * 64
                        K.op("dve", lambda h, pos=pos, p0=p0: h.scalar_tensor_tensor(
                            out=QdT[p0:p0 + 64, pos, :, :], in0=pF[p0:p0 + 64, 0:2, :], scalar=0.125, in1=Eb[p0:p0 + 64, :, :],
                            op0=ALU.mult, op1=ALU.mult), reads=[bF, b_Eb], writes=[b_QdT])
                    K.op("dve", lambda h: h.tensor_tensor(out=KdT[:], in0=pF[:, 2:4, :], in1=Enb[:], op=ALU.mult),
                         reads=[bF, b_Enb], writes=[b_KdT])

                def o_KeTt():
                    for f in range(2):
                        K.op("pe", lambda h, f=f: h.transpose(pT[:, f * 128:(f + 1) * 128], KeT[:, f, :], ident_sb[:]),
                             reads=[b_KeT, b_ident], writes=[bT])

                def o_Ke():
                    K.op("dve", lambda h: h.tensor_copy(out=Ke[:], in_=pT[:, 0:256]), reads=[bT], writes=[b_Ke])

                def o_v():
                    proj_tok(j, C_GV, 0)

                def o_vsb():
                    K.op("dve", lambda h: h.tensor_copy(out=vsb[:], in_=pP[0][:, :]), reads=[bP[0]], writes=[b_vsb])

                def o_r():
                    proj_tok(j, C_GR, 1)

                def o_rexp():
                    K.op("act", lambda h: h.activation(out=sr[:], in_=pP[1][:, :], func=AF.Exp, scale=-1.0), reads=[bP[1]], writes=[b_sr])

                def o_rln():
                    K.op("act", lambda h: h.activation(out=sr[:], in_=sr[:], func=AF.Ln, bias=1.0, scale=1.0), reads=[b_sr], writes=[b_sr])
                    K.op("act", lambda h: h.activation(out=sr[:], in_=sr[:], func=AF.Exp, scale=-1.0), reads=[b_sr], writes=[b_sr])

                def o_silu():
                    K.op("dve", lambda h: h.tensor_tensor(out=sr[:], in0=pP[1][:, :], in1=sr[:], op=ALU.mult),
                         reads=[bP[1], b_sr], writes=[b_sr])
                    K.op("pool", lambda h: h.tensor_tensor(out=sr[:], in0=sr[:], in1=gg[:], op=ALU.mult),
                         reads=[b_sr, b_small], writes=[b_sr])

                def o_AT():
                    for hh in range(4):
                        ch = hh // 2
                        K.op("pe", lambda h, hh=hh, ch=ch: h.matmul(
                            pF[:, hh, :], lhsT=KdT[:, ch, :], rhs=QdT[:, hh % 2, ch, :], start=True, stop=True),
                            reads=[b_KdT, b_QdT, b_KeT], writes=[bF])

                def o_At():
                    K.op("dve", lambda h: h.tensor_tensor(out=At[:], in0=pF[:], in1=tri_sb[:], op=ALU.mult),
                         reads=[bF, b_cst], writes=[b_At])

                def o_o():
                    for hh in range(4):
                        ch = hh // 2
                        K.op("pe", lambda h, hh=hh, ch=ch: h.matmul(
                            pGo[:, hh, :], lhsT=QdT[:, hh % 2, ch, :], rhs=Sbf[:, ch, :], start=True, stop=False),
                            reads=[b_QdT, b_Sbf], writes=[bP[0]])
                        K.op("pe", lambda h, hh=hh: h.matmul(
                            pGo[:, hh, :], lhsT=At[:, hh, :], rhs=vsb[:, hh * 128:(hh + 1) * 128], start=False, stop=True),
                            reads=[b_At, b_vsb], writes=[bP[0]])

                def o_osq():
                    K.op("act", lambda h: h.activation(out=sq[:], in_=pP[0][:, :], func=AF.Square),
                         reads=[bP[0]], writes=[b_sq])

                def o_ored():
                    K.op("dve", lambda h: h.tensor_reduce(out=ss4[:, 0:4], in_=sq[:].rearrange("p (a b) -> p a b", a=4),
                                                          axis=AX.X, op=ALU.add), reads=[b_sq], writes=[b_ss4])
                    K.op("act", lambda h: h.activation(out=ss4[:, 0:4], in_=ss4[:, 0:4], func=AF.Ln, scale=1.0 / 128, bias=epsc[:, 0:1]),
                         reads=[b_ss4], writes=[b_ss4])
                    K.op("act", lambda h: h.activation(out=ss4[:, 4:8], in_=ss4[:, 0:4], func=AF.Exp, scale=-0.5),
                         reads=[b_ss4], writes=[b_ss4])

                def o_mix():
                    for hh in range(4):
                        K.op("dve", lambda h, hh=hh: h.scalar_tensor_tensor(
                            out=mixb[:, hh * 128:(hh + 1) * 128], in0=pGo[:, hh, :], scalar=ss4[:, 4 + hh:5 + hh],
                            in1=sr[:, hh * 128:(hh + 1) * 128], op0=ALU.mult, op1=ALU.mult),
                            reads=[bP[0], b_ss4, b_sr], writes=[bmix])

                def o_st():
                    for hh in range(4):
                        ch = hh // 2
                        K.op("pe", lambda h, hh=hh, ch=ch: h.matmul(
                            pG[:, hh * 128:(hh + 1) * 128], lhsT=Ke[:, ch * 128:(ch + 1) * 128],
                            rhs=vsb[:, hh * 128:(hh + 1) * 128], start=True, stop=True),
                            reads=[b_Ke, b_vsb], writes=[bG])

                def o_S():
                    for hh in range(4):
                        pr = (hh % 2) * 64; ch = hh // 2
                        K.op("dve", lambda h, hh=hh, pr=pr, ch=ch: h.scalar_tensor_tensor(
                            out=S[pr:pr + 64, ch, :], in0=S[pr:pr + 64, ch, :], scalar=Eb[pr:pr + 64, ch, 127:128],
                            in1=pG[pr:pr + 64, hh * 128:(hh + 1) * 128],
                            op0=ALU.mult, op1=ALU.add), reads=[b_S, b_Eb, bG], writes=[b_S])
                    if j == HALO:
                        K.op("dve", lambda h: h.tensor_scalar(out=S[:], in0=S[:], scalar1=flag_sb[:, 0:1], scalar2=None, op0=ALU.mult),
                             reads=[b_S, b_flag], writes=[b_S])
                    K.op("pool", lambda h: h.tensor_copy(out=Sbf[:], in_=S[:]), reads=[b_S], writes=[b_Sbf])

                def mk_qk(nm, col0, o8):
                    def o_p():
                        proj_tok(j, col0, 1)

                    def o_sq():
                        K.op("act", lambda h: h.activation(out=sq[:], in_=pP[1][:, :], func=AF.Square), reads=[bP[1]], writes=[b_sq])

                    def o_red():
                        K.op("dve", lambda h: h.tensor_reduce(out=ss8[:, o8:o8 + 8], in_=sq[:].rearrange("p (a b) -> p a b", a=8),
                                                              axis=AX.X, op=ALU.add), reads=[b_sq], writes=[b_ss8])
                        K.op("act", lambda h: h.activation(out=ss8[:, o8:o8 + 8], in_=ss8[:, o8:o8 + 8], func=AF.Ln, scale=1.0 / 64,
                                                           bias=epsc[:, 0:1]), reads=[b_ss8], writes=[b_ss8])
                        K.op("act", lambda h: h.activation(out=ss8[:, o8:o8 + 8], in_=ss8[:, o8:o8 + 8], func=AF.Exp, scale=-0.5),
                             reads=[b_ss8], writes=[b_ss8])

                    def o_n():
                        K.op("dve", lambda h: h.tensor_tensor(
                            out=qkn[:].rearrange("p (a b) -> p a b", a=8), in0=pP[1][:, :].rearrange("p (a b) -> p a b", a=8),
                            in1=ss8[:, o8:o8 + 8].rearrange("p (a o) -> p a o", o=1).broadcast_to([128, 8, 64]), op=ALU.mult),
                            reads=[bP[1], b_ss8], writes=[b_qkn])

                    def o_T():
                        for c in range(4):
                            K.op("pe", lambda h, c=c: h.transpose(pT[:, c * 128:(c + 1) * 128], qkn[:, c * 128:(c + 1) * 128], ident_sb[:]),
                                 reads=[b_qkn, b_ident], writes=[bT])

                    def o_ev():
                        if nm == "k":
                            K.op("dve", lambda h: h.tensor_scalar(out=KT[sl][:].rearrange("p a b -> p (a b)"), in0=pT[:, 0:512],
                                                                  scalar1=gk_sb[:, 0:1], scalar2=None, op0=ALU.mult),
                                 reads=[bT, b_small], writes=[b_KT[sl]])
                        else:
                            for pos in range(2):
                                p0 = pos * 64
                                K.op("dve", lambda h, pos=pos, p0=p0: h.tensor_scalar(
                                    out=qb[p0:p0 + 64, pos, :, :], in0=pT[p0:p0 + 64, 0:512].rearrange("p (a b) -> p a b", a=4),
                                    scalar1=gq_sb[p0:p0 + 64, 0:1], scalar2=0.125, op0=ALU.mult, op1=ALU.mult),
                                    reads=[bT, b_small], writes=[b_QT[j % PQT]])
                    return o_p, o_sq, o_red, o_n, o_T, o_ev

                o_k, o_ksq, o_kred, o_kn, o_kT, o_KT = mk_qk("k", C_AK, 0)
                o_q, o_qsq, o_qred, o_qn, o_qT, o_QT = mk_qk("q", C_AQ, 8)

                def o_av():
                    proj_tok(j, C_AV, 0)

                def o_VA():
                    K.op("dve", lambda h: h.tensor_copy(out=va[:, :, 0:64], in_=pP[0][:, :].rearrange("p (a b) -> p a b", a=8)),
                         reads=[bP[0]], writes=[bva])
                    K.op("pool", lambda h: h.memset(va[:, :, 64:65], 1.0), writes=[bva])
                    if j < NM:
                        K.op("dve", lambda h: h.tensor_scalar(out=va[:], in0=va[:], scalar1=flag_sb[:, 0:1], scalar2=None,
                                                              op0=ALU.mult), reads=[bva, b_flag], writes=[bva])

                def o_warm():
                    if kind == "FULL":
                        return
                    for _ in range(9):
                        K.op("pe", lambda h: h.matmul(pS[0][:, :], lhsT=ident_sb[:], rhs=dn_sb[:, 0:4, :].rearrange("p a b -> p (a b)"),
                                                      start=True, stop=True), reads=[b_ident, b_cst], writes=[bS[0]])

                G_OPS = {o_warm, o_norm, o_xn, o_xT, o_hT, o_glr, o_gk, o_v, o_glrT, o_vsb, o_z, o_ge, o_scan, o_EbEe, o_KeT, o_KeTt, o_Ke, o_st, o_S}
                GK_OPS = G_OPS | {o_k, o_ksq, o_kred, o_kn, o_kT, o_KT, o_av, o_VA}
                sched = [
                    [o_norm], [], [o_xn], [], [o_xT], [], [o_hT], "A", [],
                    [o_glr, o_gk, o_gq], [o_v], [o_glrT, o_vsb], [o_r, o_z], [o_ge, o_rexp, o_warm], [o_scan, o_rln, o_warm],
                    [o_EbEe, o_Enb, o_warm], [o_KeT, o_QdKd, o_silu, o_warm], [o_KeTt, o_k], [o_Ke, o_AT, o_ksq], [o_At, o_kred],
                    [o_o, o_st], [o_osq, o_kn, o_S], [o_ored, o_kT], [o_q, o_KT], [o_mix, o_qsq], [o_av, o_qred],
                    [o_VA, o_qn], [o_qT], [o_QT],
                ]
                allowed = None if kind == "FULL" else (GK_OPS if kind == "GK" else G_OPS)
                for step in sched:
                    if step == "A":
                        yield "A"
                        continue
                    did = False
                    for f in step:
                        if allowed is None or f in allowed:
                            f(); did = True
                    if did or kind == "FULL":
                        yield None

            def attn(j, filler):
                mixb, bmix = mixs[j % 2], bmixs[j % 2]
                xb, bxb = xt[j % 3], b_xt[j % 3]
                qb, bqb = QT[j % PQT], b_QT[j % PQT]
                groups = []
                for hh in range(8):
                    ents = [(i, g, m) for i, (g, m) in enumerate(ENTRIES) if j - m >= KV0]
                    k0 = 0
                    while k0 < len(ents):
                        grp = ents[k0:k0 + 4]
                        groups.append((hh, grp, k0 == 0, k0 + 4 >= len(ents)))
                        k0 += 4
                state = {"n": 0}

                def issue_qk(gi):
                    hh, grp, first, last = groups[gi]
                    pr = (hh % 2) * 64; ch = hh // 2
                    sbk = gi % 2
                    contiguous = all(grp[t][0] == grp[0][0] + t for t in range(len(grp)))
                    n = len(grp)
                    if contiguous:
                        i0 = grp[0][0]
                        K.op("pe", lambda h, hh=hh, i0=i0, n=n, sbk=sbk: h.matmul(
                            pS[sbk][:, 0:n * 128], lhsT=nsI_sb[:, hh, :], rhs=dn_sb[:, i0:i0 + n, :].rearrange("p a b -> p (a b)"),
                            start=True, stop=False), reads=[b_cst], writes=[bS[sbk]])
                    else:
                        for t, (i, g, m) in enumerate(grp):
                            K.op("pe", lambda h, hh=hh, i=i, t=t, sbk=sbk: h.matmul(
                                pS[sbk][:, t * 128:(t + 1) * 128], lhsT=nsI_sb[:, hh, :], rhs=dn_sb[:, i, :],
                                start=(t == 0), stop=False), reads=[b_cst], writes=[bS[sbk]])
                    for t, (i, g, m) in enumerate(grp):
                        ks = (j - m) % KVRING
                        K.op("pe", lambda h, t=t, ks=ks, hh=hh, ch=ch, sbk=sbk, n=n, qb=qb: h.matmul(
                            pS[sbk][:, t * 128:(t + 1) * 128], lhsT=KT[ks][:, ch, :], rhs=qb[:, hh % 2, ch, :],
                            start=False, stop=(t == n - 1)), reads=[b_KT[ks], bqb], writes=[bS[sbk]])
                    pb = gi % 3
                    K.op("act", lambda h, sbk=sbk, pb=pb, n=n: h.activation(out=Pt[pb][:, 0:n * 128], in_=pS[sbk][:, 0:n * 128], func=AF.Exp),
                         reads=[bS[sbk]], writes=[b_Pt[pb]])

                def issue_pv(gi):
                    hh, grp, first, last = groups[gi]
                    pb = gi % 3
                    n = len(grp)
                    for t, (i, g, m) in enumerate(grp):
                        ks = (j - m) % KVRING
                        K.op("pe", lambda h, hh=hh, t=t, ks=ks, pb=pb, st=(first and t == 0), sp=(last and t == n - 1): h.matmul(
                            pO[:, hh % 4, 0:65], lhsT=Pt[pb][:, t * 128:(t + 1) * 128], rhs=VA[ks][:, hh, :],
                            start=st, stop=sp), reads=[b_Pt[pb], b_VA[ks]], writes=[bO])

                def finish_half(half):
                    K.op("dve", lambda h, half=half: h.tensor_scalar(out=rden[:, half * 4:half * 4 + 4], in0=pO[:, :, 64], scalar1=1e-30,
                                                                     scalar2=None, op0=ALU.add), reads=[bO], writes=[b_rden])
                    K.op("dve", lambda h, half=half: h.reciprocal(out=rden[:, half * 4:half * 4 + 4], in_=rden[:, half * 4:half * 4 + 4]),
                         reads=[b_rden], writes=[b_rden])
                    for t in range(4):
                        hh = half * 4 + t
                        K.op("dve", lambda h, hh=hh, t=t: h.tensor_scalar(
                            out=mixb[:, 512 + hh * 64:512 + (hh + 1) * 64], in0=pO[:, t, 0:64], scalar1=rden[:, hh:hh + 1],
                            scalar2=None, op0=ALU.mult), reads=[bO, b_rden], writes=[bmix])

                ng = len(groups)
                LAG = 2
                for gi in range(ng + LAG):
                    if gi < ng:
                        issue_qk(gi)
                    if gi >= LAG:
                        issue_pv(gi - LAG)
                        hh_prev, _, _, last_prev = groups[gi - LAG]
                        if last_prev and hh_prev % 4 == 3:
                            finish_half(hh_prev // 4)
                    if filler is not None and gi >= 1:
                        next(filler, None)

                if filler is not None:
                    for _ in filler:
                        pass

            def tail(j):
                mixb, bmix = mixs[j % 2], bmixs[j % 2]
                xb, bxb = xt[j % 3], b_xt[j % 3]
                for c in range(8):
                    K.op("pe", lambda h, c=c: h.transpose(pT[:, c * 128:(c + 1) * 128], mixb[:, c * 128:(c + 1) * 128], ident_sb[:]),
                         reads=[bmix, b_ident], writes=[bT])
                yield
                yield
                K.op("dve", lambda h: h.tensor_copy(out=mixT[:].rearrange("p a b -> p (a b)"), in_=pT[:, :]), reads=[bT], writes=[b_mixT])
                yield
                yield
                x1b, bx1 = xb, bxb
                for hf in range(2):
                    for c in range(8):
                        K.op("pe", lambda h, c=c, hf=hf: h.matmul(
                            pP[hf][:, :], lhsT=mixT[:, c, :], rhs=wout[:, c, hf * 512:(hf + 1) * 512],
                            start=(c == 0), stop=(c == 7)), reads=[b_mixT, b_wout], writes=[bP[hf]])
                    K.op("dve", lambda h, hf=hf, x1b=x1b, xb=xb: h.tensor_tensor(
                        out=x1b[:, hf * 512:(hf + 1) * 512], in0=pP[hf][:, :], in1=xb[:, hf * 512:(hf + 1) * 512], op=ALU.add),
                        reads=[bP[hf], bxb], writes=[bx1])
                    yield
                qi = j - HALO
                K.dma("sp", lambda h, x1b=x1b, qi=qi: h.dma_start(out=x1d[qi * 128:(qi + 1) * 128, :], in_=x1b[:]),
                      reads=[bx1], writes=[])
                load_x(j + 3)

            def run_all(gen):
                if gen is not None:
                    for _ in gen:
                        pass

            import itertools
            pending = None
            pending_tail = None
            pre = [j for j in range(NL if STOP >= 1 else 0) if j < HALO]
            def adv_A(gen):
                for v in gen:
                    if v == "A":
                        break

            cur = None
            if pre:
                cur = front(pre[0]); adv_A(cur)
            for idx, j in enumerate(pre):
                nxt = None
                if idx + 1 < len(pre):
                    nxt = front(pre[idx + 1]); adv_A(nxt)
                run_all(cur)
                cur = nxt
            for j in range(NL if STOP >= 1 else 0):
                kind_j = "FULL" if j >= HALO else ("GK" if j >= KV0 else "G")
                if kind_j != "FULL":
                    continue
                if j == HALO:
                    run_all(front(j))
                    load_x(j + 1)
                    load_x(j + 2)
                else:
                    run_all(pending)
                pending = front(j + 1) if j + 1 < NL else None
                attn(j, itertools.chain(*[g for g in (pending_tail, pending) if g is not None]))
                pending_tail = tail(j)
            run_all(pending_tail)
            K.barrier()

        with ExitStack() as e2:
            def sb2(name, shape, dt=F32):
                return e2.enter_context(nc.sbuf_tensor(name, list(shape), dt))
            GT = 384
            NT = GT // 128
            wup = sb2("wup", [128, 8, 2 * DFF], BF16); b_wups = [Buf() for _ in range(8)]
            wdn = sb2("wdn", [128, NGC, D], BF16); b_wdn = Buf()
            cw = sb2("cw", [128, 3, NFC]); cb = sb2("cb", [128, NFC]); b_cw = Buf()
            dma_in("sp", cw[:], cwT, b_cw); dma_in("sp", cb[:], cbT, b_cw)
            w_up_v = w_up.rearrange("(c p) n -> p c n", p=128)
            for i in (0, 4, 1, 5, 2, 6, 3, 7):
                c0 = i * 704
                K.dma("pool", lambda h, c0=c0: h.dma_start(out=wup[:, :, c0:c0 + 704], in_=w_up_v[:, :, c0:c0 + 704]),
                      writes=[b_wups[i]])
            pU = [pP[0], pP[1], pS[0], pS[1]]; bU = [bP[0], bP[1], bS[0], bS[1]]
            pD = [pF, pO]; bD = [bF, bO]
            x1a = [sb2("x1a%d" % i, [128, D]) for i in range(3)]; b_x1a = [Buf(), Buf(), Buf()]
            x1r = [sb2("x1r%d" % i, [128, D]) for i in range(2)]; b_x1r = [Buf(), Buf()]
            stg2 = x1r; b_stg2 = b_x1r

            def wdn_loader():
                for i in range(NGC):
                    s_ = i % 2
                    dma_in("sp", stg2[s_][:], w_down[i * 128:(i + 1) * 128, :], b_stg2[s_])
                    if i >= 1:
                        yield
                    K.op("dve", lambda h, s_=s_, i=i: h.tensor_tensor(out=wdn[:, i, :], in0=stg2[s_][:], in1=g2b[:], op=ALU.mult),
                         reads=[b_stg2[s_], b_gb], writes=[b_wdn])
                    yield
            wdn_gen = wdn_loader()
            ssb = sb2("ssb", [128, 12]); b_ssbs = [Buf(), Buf(), Buf()]
            xn2 = [sb2("xn2_%d" % i, [128, D], BF16) for i in range(3)]; b_xn2 = [Buf(), Buf(), Buf()]
            h2T = [sb2("h2T%d" % i, [128, 8, GT], BF16) for i in range(1)] * 2; b_h2T = [Buf()] * 2
            NU = 4
            usb = [sb2("usb%d" % i, [128, GT + 2]) for i in range(NU)]; b_usb = [Buf() for _ in range(NU)]; b_ush = [Buf() for _ in range(NU)]
            NY = 5
            ysb = [sb2("ysb%d" % i, [128, GT]) for i in range(NY)]; b_ysb = [Buf() for _ in range(NY)]
            sg = [sb2("sg%d" % i, [128, GT]) for i in range(2)]; b_sg = [Buf(), Buf()]
            mT = [sb2("mT%d" % i, [128, NGC, GT], BF16) for i in range(1)]; b_mT = [Buf()]
            hal = sb2("hal", [128, NFC, 2]); b_hal = [Buf() for _ in range(NFC)]
            K.op("pool", lambda h: h.memset(hal[:], 0.0), writes=b_hal)

            groups2 = [[0]] + [list(range(q0, min(q0 + NT, NM + 1))) for q0 in range(1, NM + 1, NT)]
            if STOP < 2:
                groups2 = []
            tcnt = {"n": 0}

            loaded_g = set()

            def load_x1(gidx):
                if gidx in loaded_g or gidx >= len(groups2):
                    return
                loaded_g.add(gidx)
                for qi in groups2[gidx]:
                    K.dma("sp", lambda h, qi=qi: h.dma_start(out=x1a[qi % 3][:], in_=x1d[qi * 128:(qi + 1) * 128, :]),
                          writes=[b_x1a[qi % 3]])

            def front2_a(gidx):
                tiles = groups2[gidx]
                load_x1(gidx)
                for t, qi in enumerate(tiles):
                    xa, bxa = x1a[qi % 3], b_x1a[qi % 3]
                    xnb, bxn = xn2[qi % 3], b_xn2[qi % 3]
                    o4 = (qi % 3) * 4
                    b_ssb = b_ssbs[qi % 3]
                    K.op("act", lambda h, xa=xa, xnb=xnb, o4=o4: h.activation(out=xnb[:], in_=xa[:], func=AF.Square, accum_out=ssb[:, o4:o4 + 1]),
                         reads=[bxa], writes=[bxn, b_ssb])
                    K.op("act", lambda h, o4=o4: h.activation(out=ssb[:, o4 + 2:o4 + 3], in_=ssb[:, o4:o4 + 1], func=AF.Ln, scale=1.0 / D, bias=epsc[:, 0:1]),
                         reads=[b_ssb], writes=[b_ssb])
                    K.op("act", lambda h, o4=o4: h.activation(out=ssb[:, o4 + 1:o4 + 2], in_=ssb[:, o4 + 2:o4 + 3], func=AF.Exp, scale=-0.5),
                         reads=[b_ssb], writes=[b_ssb])
                    K.op("dve", lambda h, xa=xa, xnb=xnb, o4=o4: h.tensor_scalar(out=xnb[:], in0=xa[:], scalar1=ssb[:, o4 + 1:o4 + 2], scalar2=None,
                                                                                 op0=ALU.mult), reads=[bxa, b_ssb], writes=[bxn])

            def front2_b(gidx, only_t=None):
                if gidx >= len(groups2):
                    return
                tiles = groups2[gidx]
                hb, bhb = h2T[gidx % 2], b_h2T[gidx % 2]
                for t, qi in enumerate(tiles):
                    if only_t is not None and t != only_t:
                        continue
                    xnb, bxn = xn2[qi % 3], b_xn2[qi % 3]
                    for c in range(8):
                        K.op("pe", lambda h, c=c, xnb=xnb: h.transpose(pT[:, c * 128:(c + 1) * 128], xnb[:, c * 128:(c + 1) * 128], ident_sb[:]),
                             reads=[bxn, b_ident], writes=[bT])
                    for c in range(8):
                        K.op("act", lambda h, c=c, hb=hb, t=t: h.activation(
                            out=hb[:, c, t * 128:(t + 1) * 128], in_=pT[:, c * 128:(c + 1) * 128], func=AF.Identity,
                            scale=G2[:, c:c + 1], bias=sh2[:, c:c + 1]), reads=[bT, b_G, b_modT], writes=[bhb])

            def upconv(gidx):
                tiles = groups2[gidx]
                is_halo = (gidx == 0)
                ntok = 128 * len(tiles)
                hb, bhb = h2T[gidx % 2], b_h2T[gidx % 2]
                mb, bmb = mT[0], b_mT[0]
                load_x1(gidx + 1)
                order = []
                for i in range(NGC):
                    order += [i, NGC + i]

                def stA(oi):
                    fc = order[oi]
                    ub, bub = pU[oi % 4], bU[oi % 4]
                    us, bus, bush = usb[oi % NU], b_usb[oi % NU], b_ush[oi % NU]
                    for c in range(8):
                        K.op("pe", lambda h, c=c, fc=fc, ub=ub: h.matmul(
                            ub[:, 0:ntok], lhsT=wup[:, c, fc * 128:(fc + 1) * 128], rhs=hb[:, c, 0:ntok],
                            start=(c == 0), stop=(c == 7)), reads=[bhb] + [b_wups[c_] for c_ in range((fc * 128) // 704, (fc * 128 + 127) // 704 + 1)], writes=[bub])
                    K.op("pool", lambda h, fc=fc, us=us: h.tensor_copy(out=us[:, 0:2], in_=hal[:, fc, :]),
                         reads=[b_hal[fc]], writes=[bush])
                    K.op("act", lambda h, ub=ub, us=us: h.copy(out=us[:, 2:2 + ntok], in_=ub[:, 0:ntok]),
                         reads=[bub], writes=[bus])
                    K.op("pool", lambda h, fc=fc, us=us: h.tensor_copy(out=hal[:, fc, :], in_=us[:, ntok:ntok + 2]),
                         reads=[bus], writes=[b_hal[fc]])

                def stB(oi):
                    fc = order[oi]
                    us, bus, bush = usb[oi % NU], b_usb[oi % NU], b_ush[oi % NU]
                    yb, byb = ysb[oi % NY], b_ysb[oi % NY]
                    K.op("act", lambda h, fc=fc, us=us, yb=yb: h.activation(
                        out=yb[:, 0:ntok], in_=us[:, 2:2 + ntok], func=AF.Identity, scale=cw[:, 2, fc:fc + 1], bias=cb[:, fc:fc + 1]),
                        reads=[bus, b_cw], writes=[byb])
                    K.op("dve", lambda h, fc=fc, us=us, yb=yb: h.scalar_tensor_tensor(
                        out=yb[:, 0:ntok], in0=us[:, 1:1 + ntok], scalar=cw[:, 1, fc:fc + 1], in1=yb[:, 0:ntok],
                        op0=ALU.mult, op1=ALU.add), reads=[bus, bush, b_cw, byb], writes=[byb])
                    K.op("dve", lambda h, fc=fc, us=us, yb=yb: h.scalar_tensor_tensor(
                        out=yb[:, 0:ntok], in0=us[:, 0:ntok], scalar=cw[:, 0, fc:fc + 1], in1=yb[:, 0:ntok],
                        op0=ALU.mult, op1=ALU.add), reads=[bus, bush, b_cw, byb], writes=[byb])

                def stC1(i):
                    yg, byg = ysb[(2 * i) % NY], b_ysb[(2 * i) % NY]
                    sgb, bsg = sg[i % 2], b_sg[i % 2]
                    K.op("act", lambda h, yg=yg, sgb=sgb: h.activation(out=sgb[:, 0:ntok], in_=yg[:, 0:ntok], func=AF.Silu),
                         reads=[byg], writes=[bsg])

                def stC2(i):
                    yv, byv = ysb[(2 * i + 1) % NY], b_ysb[(2 * i + 1) % NY]
                    sgb, bsg = sg[i % 2], b_sg[i % 2]
                    K.op("pool", lambda h, i=i, sgb=sgb, yv=yv: h.tensor_tensor(
                        out=mb[:, i, 0:ntok], in0=sgb[:, 0:ntok], in1=yv[:, 0:ntok], op=ALU.mult),
                        reads=[bsg, byv], writes=[bmb])

                n = len(order)
                for oi in range(n + 5):
                    if is_halo:
                        next(wdn_gen, None)
                    if oi == 20 and gidx + 1 < len(groups2):
                        front2_a(gidx + 1)
                    if oi >= n and (oi - n) % 2 == 0:
                        front2_b(gidx + 1, only_t=(oi - n) // 2)
                    if oi < n:
                        stA(oi)
                    if (not is_halo) and 0 <= oi - 1 < n:
                        stB(oi - 1)
                    if (not is_halo) and 0 <= oi - 3 < n and (oi - 3) % 2 == 0:
                        stC1((oi - 3) // 2)
                    if (not is_halo) and 0 <= oi - 5 < n and (oi - 5) % 2 == 0:
                        stC2((oi - 5) // 2)
                if is_halo:
                    K.op("dve", lambda h: h.tensor_scalar(out=hal[:], in0=hal[:], scalar1=flag_sb[:, 0:1], scalar2=None, op0=ALU.mult),
                         reads=b_hal + [b_flag], writes=b_hal)

            def down(gidx):
                tiles = groups2[gidx]
                mb, bmb = mT[0], b_mT[0]
                for t, qi in enumerate(tiles):
                    xr, bxr = x1r[qi % 2], b_x1r[qi % 2]
                    K.dma("sp", lambda h, xr=xr, qi=qi: h.dma_start(out=xr[:], in_=x1d[qi * 128:(qi + 1) * 128, :]), writes=[bxr])
                    for hf in range(2):
                        db, bdb = pD[hf], bD[hf]
                        dflat = db[:].rearrange("p a b -> p (a b)")
                        for i in range(NGC):
                            K.op("pe", lambda h, i=i, hf=hf, dflat=dflat, t=t: h.matmul(
                                dflat, lhsT=mb[:, i, t * 128:(t + 1) * 128], rhs=wdn[:, i, hf * 512:(hf + 1) * 512],
                                start=(i == 0), stop=(i == NGC - 1)), reads=[bmb, b_wdn], writes=[bdb])
                        K.op("dve", lambda h, hf=hf, dflat=dflat, xr=xr: h.tensor_tensor(
                            out=xr[:, hf * 512:(hf + 1) * 512], in0=dflat, in1=xr[:, hf * 512:(hf + 1) * 512], op=ALU.add),
                            reads=[bdb, bxr], writes=[bxr])
                    oi_ = qi - 1
                    K.dma("sp", lambda h, xr=xr, oi_=oi_: h.dma_start(out=out[oi_ * 128:(oi_ + 1) * 128, :], in_=xr[:]),
                          reads=[bxr], writes=[])

            if groups2:
                front2_a(0)
                front2_b(0)
            for gidx in range(len(groups2)):
                upconv(gidx)
                if gidx == 0:
                    for _ in wdn_gen:
                        pass
                if gidx > 0:
                    down(gidx)
            K.barrier()

        with nc.Block() as block:
            @block.tensor
            def _(e):
                K.emit("pe", e)

            @block.scalar
            def _(e):
                K.emit("act", e)

            @block.vector
            def _(e):
                K.emit("dve", e)

            @block.gpsimd
            def _(e):
                K.emit("pool", e)

            @block.sync
            def _(e):
                K.emit("sp", e)
    return nc


def _consts():
    bf = ml_dtypes.bfloat16
    ident = np.eye(128, dtype=np.float32).astype(bf)
    slopes = np.array([2.0 ** (-8.0 * (h + 1) / 8) for h in range(8)], dtype=np.float32)
    nsI = np.zeros((128, 8, 128), np.float32)
    for h in range(8):
        nsI[np.arange(128), h, np.arange(128)] = -slopes[h]
    k = np.arange(128)[:, None]
    q = np.arange(128)[None, :]
    dnm = np.zeros((128, 24, 128), np.float32)
    for i, (g, m) in enumerate(ENTRIES):
        d = DILS[g]
        dist = 128 * m + q - k
        valid = (dist >= 0) & (dist % d == 0) & (dist <= 128 * d)
        dnm[:, i, :] = np.where(valid, dist, BIGD)
    tri = (k <= q).astype(np.float32)
    tri4 = np.repeat(tri[:, None, :], 4, axis=1)
    return ident, nsI.astype(bf), dnm.astype(bf), np.ascontiguousarray(tri4)


def make_in_maps(inputs, NM):
    S_half = NM * 128
    x = np.asarray(inputs["x"], np.float32)
    B = x.shape[0]
    ident, nsI, dnm, tri4 = _consts()
    f = lambda a: np.ascontiguousarray(np.asarray(a, np.float32))
    shared = {
        "w_ada": f(inputs["w_ada"][0]), "b_ada": f(inputs["b_ada"][0]).reshape(1, -1),
        "g1row": f(inputs["norm1_g"][0]).reshape(1, -1), "g2row": f(inputs["norm2_g"][0]).reshape(1, -1),
        "w_in": f(inputs["w_in"][0]), "wg": f(inputs["gla_w_gate"][0]),
        "nbgT": f(-np.asarray(inputs["gla_b_gate"][0], np.float32).reshape(2, 128).T),
        "ggb": f(np.tile(np.asarray(inputs["gla_norm_g"][0], np.float32)[None, :], (128, 4))),
        "gqc": f(np.tile(np.asarray(inputs["q_norm_g"][0], np.float32), 2).reshape(128, 1)),
        "gkc": f(np.tile(np.asarray(inputs["k_norm_g"][0], np.float32), 2).reshape(128, 1)),
        "w_out": f(inputs["w_out"][0]), "w_up": f(inputs["w_up"][0]),
        "cwT": f(np.asarray(inputs["conv_w"][0], np.float32).reshape(3, NFC, 128).transpose(2, 0, 1)),
        "cbT": f(np.asarray(inputs["conv_b"][0], np.float32).reshape(NFC, 128).T),
        "w_down": f(inputs["w_down"][0]),
        "ident": ident, "nsI": nsI, "dn": dnm, "tri4": tri4,
    }
    maps = []
    for core in range(2 * B):
        b, half = core // 2, core % 2
        if half == 0:
            xloc = np.concatenate([np.zeros((S_half, D), np.float32), x[b, :S_half]], axis=0)
        else:
            xloc = x[b, :2 * S_half]
        m = dict(shared)
        m["xl"] = np.ascontiguousarray(xloc)
        m["cT"] = f(np.asarray(inputs["c"], np.float32)[b].reshape(128, 8))
        m["flag"] = np.full((128, 1), float(half), np.float32)
        maps.append(m)
    return maps


_NC_CACHE = {}


def run(inputs, NM, debug=False):
    key = (NM, debug)
    if key not in _NC_CACHE:
        _NC_CACHE[key] = build(NM, debug)
    nc = _NC_CACHE[key]
    maps = make_in_maps(inputs, NM)
    res = run_bass_kernel_spmd(nc, maps, core_ids=list(range(len(maps))))
    return res


def kernel(**inputs):
    NM = 32
    res = run(inputs, NM)
    B = np.asarray(inputs["x"]).shape[0]
    outp = np.zeros((B, 2 * NM * 128, D), np.float32)
    for core in range(2 * B):
        b, half = core // 2, core % 2
        outp[b, half * NM * 128:(half + 1) * NM * 128] = np.asarray(res.results[core]["out"], np.float32)
    return outp
```

```python
import os
from contextlib import ExitStack
import numpy as np
import ml_dtypes
import concourse.bass as bass
import concourse.mybir as mybir
from concourse.bass_utils import run_bass_kernel_spmd

F32 = mybir.dt.float32
BF16 = mybir.dt.bfloat16
ALU = mybir.AluOpType
AF = mybir.ActivationFunctionType
AX = mybir.AxisListType

D = 1024
NCH = 8
IN_W = 3088
DFF = 2816
NFC = 44
NGC = 22
EPS = 1e-6
C_GQ, C_GK, C_GV, C_GR, C_GLR, C_AQ, C_AK, C_AV = 0, 256, 512, 1024, 1536, 1552, 2064, 2576
DILS = (1, 4, 16)
BIGD = float(2 ** 18)
ENTRIES = [(g, m) for g in range(3) for m in range(DILS[g] + 1)]
KVRING = 18


class Buf:
    __slots__ = ("w", "r", "name")

    def __init__(self, name=""):
        self.w = None
        self.r = {}
        self.name = name


class Sched:
    def __init__(self, nc, sems, dma_sems):
        self.nc = nc
        self.engs = {}
        for n in ("pe", "act", "dve", "pool", "sp"):
            self.engs[n] = {"sem": sems[n], "cnt": 0, "waited": {}, "ops": []}
        self.dma_sems = dma_sems
        self.dma_state = [0] * len(dma_sems)
        self.dma_rr = 0
        self.n_sw = 4
        self.dma_rr_sw = 0
        self.semobj = {}
        for n in sems:
            self.semobj[id(sems[n])] = sems[n]
        for s in dma_sems:
            self.semobj[id(s)] = s

    def _deps(self, eng, reads, writes):
        deps = {}

        def add(tok):
            if tok is None:
                return
            s, v = tok
            if deps.get(s, 0) < v:
                deps[s] = v
        for b in reads:
            add(b.w)
        for b in writes:
            add(b.w)
            for s, v in b.r.items():
                add((s, v))
        e = self.engs[eng]
        waits = []
        for s, v in deps.items():
            if eng == "pe" and s == id(e["sem"]):
                continue
            if e["waited"].get(s, 0) < v:
                e["waited"][s] = v
                waits.append((self.semobj[s], v))
        return waits

    def _commit(self, tok, reads, writes):
        s, v = tok
        for b in writes:
            b.w = tok
            b.r = {}
        for b in reads:
            if b.r.get(s, 0) < v:
                b.r[s] = v

    def op(self, eng, fn, reads=(), writes=()):
        e = self.engs[eng]
        waits = self._deps(eng, reads, writes)
        e["cnt"] += 1
        tok = (id(e["sem"]), e["cnt"])
        e["ops"].append((waits, fn, e["sem"], 1))
        self._commit(tok, reads, writes)

    def dma(self, eng, fn, reads=(), writes=()):
        e = self.engs[eng]
        if eng == "pool":
            i = self.dma_rr_sw
            self.dma_rr_sw = (self.dma_rr_sw + 1) % self.n_sw
        else:
            i = self.n_sw + self.dma_rr
            self.dma_rr = (self.dma_rr + 1) % (len(self.dma_sems) - self.n_sw)
        sem = self.dma_sems[i]
        waits = self._deps(eng, reads, writes)
        prev = self.dma_state[i]
        if prev > 0 and e["waited"].get(id(sem), 0) < prev:
            e["waited"][id(sem)] = prev
            waits.append((sem, prev))
        self.dma_state[i] = prev + 16
        tok = (id(sem), prev + 16)
        e["ops"].append((waits, fn, sem, 16))
        self._commit(tok, reads, writes)

    def barrier(self):
        toks = []
        for n, e in self.engs.items():
            if e["cnt"] > 0:
                toks.append((e["sem"], e["cnt"]))
        for i, s in enumerate(self.dma_sems):
            if self.dma_state[i] > 0:
                toks.append((s, self.dma_state[i]))
        for n, e in self.engs.items():
            waits = []
            for s, v in toks:
                if s is e["sem"]:
                    continue
                if e["waited"].get(id(s), 0) < v:
                    e["waited"][id(s)] = v
                    waits.append((s, v))
            if waits:
                e["ops"].append((waits, None, None, 0))

    def emit(self, eng, h):
        for waits, fn, sem, inc in self.engs[eng]["ops"]:
            for s, v in waits:
                h.wait_ge(s, v)
            if fn is not None:
                fn(h).then_inc(sem, inc)


def build(NM, debug=False):
    STOP = int(os.environ.get('KSTOP', '2'))
    NOW = int(os.environ.get('KNOW', '1'))
    CUT = int(os.environ.get('KCUT', '9'))
    PXN = int(os.environ.get('PXN', '2')); PHT = int(os.environ.get('PHT', '2')); PQT = int(os.environ.get('PQT', '2'))
    NL = 2 * NM
    HALO = NM - 1
    KV0 = max(0, HALO - 16)
    NQ = NM + 1
    nc = bass.Bass("TRN2", target_bir_lowering=False)

    def din(name, shape, dt=F32):
        return nc.dram_tensor(name, list(shape), dt, kind="ExternalInput").ap()

    xl = din("xl", [NL * 128, D])
    cT = din("cT", [128, 8])
    w_ada = din("w_ada", [D, 6 * D])
    b_ada = din("b_ada", [1, 6 * D])
    g1row = din("g1row", [1, D])
    g2row = din("g2row", [1, D])
    w_in = din("w_in", [D, IN_W])
    wg = din("wg", [16, 256])
    nbgT = din("nbgT", [128, 2])
    ggb = din("ggb", [128, 512])
    gqc = din("gqc", [128, 1])
    gkc = din("gkc", [128, 1])
    w_out = din("w_out", [D, D])
    w_up = din("w_up", [D, 2 * DFF])
    cwT = din("cwT", [128, 3, NFC])
    cbT = din("cbT", [128, NFC])
    w_down = din("w_down", [DFF, D])
    flag = din("flag", [128, 1])
    ident = din("ident", [128, 128], BF16)
    nsI = din("nsI", [128, 8, 128], BF16)
    dn = din("dn", [128, 24, 128], BF16)
    tri4 = din("tri4", [128, 4, 128])
    out = nc.dram_tensor("out", [NM * 128, D], F32, kind="ExternalOutput").ap()
    x1d = nc.dram_tensor("x1d", [NQ * 128, D], F32,
                         kind="ExternalOutput" if debug else "Internal").ap()

    es = ExitStack()
    with es:
        def sb(name, shape, dt=F32):
            return es.enter_context(nc.sbuf_tensor(name, list(shape), dt))

        def ps(name, shape, dt=F32):
            return es.enter_context(nc.psum_tensor(name, list(shape), dt))

        sems = {n: es.enter_context(nc.semaphore("s_" + n)) for n in ("pe", "act", "dve", "pool", "sp")}
        dma_sems = [es.enter_context(nc.semaphore("s_dma%d" % i)) for i in range(24)]
        K = Sched(nc, sems, dma_sems)

        pT = ps("pT", [128, 1024], BF16); bT = Buf("T")
        pP = [ps("pP%d" % i, [128, 512]) for i in range(2)]; bP = [Buf("P0"), Buf("P1")]
        pF = ps("pF", [128, 4, 128]); bF = Buf("F")
        pG = ps("pG", [128, 512]); bG = Buf("G")
        pS = [ps("pS%d" % i, [128, 512]) for i in range(2)]; bS = [Buf("S0"), Buf("S1")]
        pO = ps("pO", [128, 4, 128]); bO = Buf("O")

        ident_sb = sb("ident_sb", [128, 128], BF16); b_ident = Buf()
        flag_sb = sb("flag_sb", [128, 1]); b_flag = Buf()
        modT = sb("modT", [128, 48]); b_modT = Buf()
        G1 = sb("G1c", [128, 8]); G2 = sb("G2c", [128, 8]); b_G = Buf()
        g1b = sb("g1b", [128, D]); g2b = sb("g2b", [128, D]); b_gb = Buf()

        def dma_in(eng, dst, src, wbuf):
            K.dma(eng, lambda h: h.dma_start(out=dst, in_=src), writes=[wbuf])

        epsc = sb("epsc", [128, 1])
        b_eps = Buf()
        K.op("pool", lambda h: h.memset(epsc[:], EPS), writes=[b_eps])
        dma_in("sp", ident_sb[:], ident, b_ident)
        dma_in("sp", flag_sb[:], flag, b_flag)

        with ExitStack() as e0:
            def sb0(name, shape, dt=F32):
                return e0.enter_context(nc.sbuf_tensor(name, list(shape), dt))
            c_sb = sb0("c_sb", [128, 8]); b_c = Buf()
            cond = sb0("cond", [128, 8]); b_cond = Buf()
            wa = [sb0("wa%d" % i, [128, 8, 512]) for i in range(3)]; b_wa = [Buf() for _ in range(3)]
            brow = sb0("brow", [1, 6 * D]); b_brow = Buf()
            mrow = sb0("mrow", [1, 6 * D]); b_mrow = Buf()
            grow = sb0("grow", [1, 2 * D]); b_grow = Buf()
            ones1 = sb0("ones1", [1, 128]); b_ones = Buf()
            dma_in("sp", c_sb[:], cT, b_c)
            dma_in("sp", brow[:], b_ada, b_brow)
            dma_in("sp", grow[:, 0:D], g1row, b_grow)
            dma_in("sp", grow[:, D:2 * D], g2row, b_grow)
            K.op("dve", lambda h: h.memset(ones1[:], 1.0), writes=[b_ones])
            K.op("act", lambda h: h.activation(out=cond[:], in_=c_sb[:], func=AF.Silu),
                 reads=[b_c], writes=[b_cond])
            w_ada_v = w_ada.rearrange("(p k) n -> p k n", k=8)
            for blk in range(12):
                s = blk % 3
                dma_in("sp", wa[s][:], w_ada_v[:, :, blk * 512:(blk + 1) * 512], b_wa[s])
                pb = pP[blk % 2]; bb = bP[blk % 2]
                for k in range(8):
                    K.op("pe", lambda h, pb=pb, s=s, k=k: h.matmul(
                        pb[0:1, :], lhsT=cond[:, k:k + 1], rhs=wa[s][:, k, :],
                        start=(k == 0), stop=(k == 7)),
                        reads=[b_cond, b_wa[s]], writes=[bb])
                K.op("dve", lambda h, pb=pb, blk=blk: h.tensor_tensor(
                    out=mrow[:, blk * 512:(blk + 1) * 512], in0=pb[0:1, :],
                    in1=brow[:, blk * 512:(blk + 1) * 512], op=ALU.add),
                    reads=[bb, b_brow], writes=[b_mrow])
            for j in range(48):
                K.op("pe", lambda h, j=j: h.matmul(
                    pG[:, j:j + 1], lhsT=mrow[0:1, j * 128:(j + 1) * 128], rhs=ones1[0:1, 0:1],
                    start=True, stop=True), reads=[b_mrow, b_ones], writes=[bG])
            K.op("dve", lambda h: h.tensor_copy(out=modT[:], in_=pG[:, 0:48]), reads=[bG], writes=[b_modT])
            for gi, (gb, col0) in enumerate(((g1b, 2 * D), (g2b, 5 * D))):
                for hf in range(2):
                    pb = pP[hf]; bb = bP[hf]
                    K.op("pe", lambda h, pb=pb, col0=col0, hf=hf: h.matmul(
                        pb[:, :], lhsT=ones1[0:1, :], rhs=mrow[0:1, col0 + hf * 512: col0 + (hf + 1) * 512],
                        start=True, stop=True), reads=[b_mrow, b_ones], writes=[bb])
                    K.op("dve", lambda h, pb=pb, gb=gb, hf=hf: h.tensor_copy(
                        out=gb[:, hf * 512:(hf + 1) * 512], in_=pb[:, :]), reads=[bb], writes=[b_gb])
            n1c = sb0("n1c", [128, 16]); b_n1c = Buf()
            for j in range(16):
                K.op("pe", lambda h, j=j: h.matmul(
                    pG[:, 64 + j:65 + j], lhsT=grow[0:1, j * 128:(j + 1) * 128], rhs=ones1[0:1, 0:1],
                    start=True, stop=True), reads=[b_grow, b_ones], writes=[bG])
            K.op("dve", lambda h: h.tensor_copy(out=n1c[:], in_=pG[:, 64:80]), reads=[bG], writes=[b_n1c])
            K.op("dve", lambda h: h.scalar_tensor_tensor(
                out=G1[:], in0=modT[:, 8:16], scalar=1.0, in1=n1c[:, 0:8], op0=ALU.add, op1=ALU.mult),
                reads=[b_modT, b_n1c], writes=[b_G])
            K.op("dve", lambda h: h.scalar_tensor_tensor(
                out=G2[:], in0=modT[:, 32:40], scalar=1.0, in1=n1c[:, 8:16], op0=ALU.add, op1=ALU.mult),
                reads=[b_modT, b_n1c], writes=[b_G])
            K.barrier()
        sh1 = modT[:, 0:8]
        sh2 = modT[:, 24:32]

        with ExitStack() as e1:
            def sb1(name, shape, dt=F32):
                return e1.enter_context(nc.sbuf_tensor(name, list(shape), dt))
            win = sb1("win", [128, 8, IN_W], BF16); b_wins = [Buf() for _ in range(4)]
            wout = sb1("wout", [128, 8, D], BF16); b_wout = Buf()
            wg_sb = sb1("wg_sb", [16, 256], BF16); b_wg = Buf()
            nbg = sb1("nbg", [128, 2]); gg = sb1("gg", [128, 512])
            gq_sb = sb1("gq_sb", [128, 1]); gk_sb = sb1("gk_sb", [128, 1]); b_small = Buf()
            nsI_sb = sb1("nsI_sb", [128, 8, 128], BF16); dn_sb = sb1("dn_sb", [128, 24, 128], BF16)
            tri_sb = sb1("tri_sb", [128, 4, 128]); b_cst = Buf()
            stg = [sb1("stg%d" % i, [128, 8, 128]) for i in range(2)]; b_stg = [Buf(), Buf()]

            w_in_v = w_in.rearrange("(c p) n -> p c n", p=128)
            for i in range(4):
                c0 = i * 772
                K.dma("pool", lambda h, c0=c0: h.dma_start(out=win[:, :, c0:c0 + 772], in_=w_in_v[:, :, c0:c0 + 772]),
                      writes=[b_wins[i]])
            K.dma("pool", lambda h: h.dma_start(out=wg_sb[:], in_=wg), writes=[b_wg])
            dma_in("sp", nbg[:], nbgT, b_small); dma_in("sp", gg[:], ggb, b_small)
            dma_in("sp", gq_sb[:], gqc, b_small); dma_in("sp", gk_sb[:], gkc, b_small)
            dma_in("sp", nsI_sb[:], nsI, b_cst); dma_in("sp", dn_sb[:], dn, b_cst)
            dma_in("sp", tri_sb[:], tri4, b_cst)
            w_out_v = w_out.rearrange("(c p) n -> p c n", p=128)
            for i in range(8):
                s = i % 2
                dma_in("sp", stg[s][:], w_out_v[:, :, i * 128:(i + 1) * 128], b_stg[s])
                K.op("dve", lambda h, s=s, i=i: h.tensor_tensor(
                    out=wout[:, :, i * 128:(i + 1) * 128], in0=stg[s][:],
                    in1=g1b[:, i * 128:(i + 1) * 128].rearrange("p (o n) -> p o n", o=1).broadcast_to([128, 8, 128]),
                    op=ALU.mult), reads=[b_stg[s], b_gb], writes=[b_wout])

            xt = [sb1("xt%d" % i, [128, D]) for i in range(3)]; b_xt = [Buf() for _ in range(3)]
            junk = sb1("junk", [128, D], BF16); b_junk = Buf()
            sq = sb1("sq", [128, 512]); b_sq = Buf()
            ss = sb1("ss", [128, 4]); b_ss = Buf()
            xn = [sb1("xn%d" % i, [128, D], BF16) for i in range(2)]; b_xn = [Buf(), Buf()]
            hT = [sb1("hT%d" % i, [128, 8, 128], BF16) for i in range(2)]; b_hT = [Buf(), Buf()]; b_hTo = [Buf(), Buf()]
            QT = [sb1("QT%d" % i, [128, 2, 4, 128], BF16) for i in range(2)]; b_QT = [Buf(), Buf()]
            KT = [sb1("KT%d" % i, [128, 4, 128], BF16) for i in range(KVRING)]; b_KT = [Buf() for _ in range(KVRING)]
            VA = [sb1("VA%d" % i, [128, 8, 65], BF16) for i in range(KVRING)]; b_VA = [Buf() for _ in range(KVRING)]
            Pt = [sb1("Pt%d" % i, [128, 512], BF16) for i in range(3)]; b_Pt = [Buf() for _ in range(3)]
            qkn = sb1("qkn", [128, 512], BF16); b_qkn = Buf()
            ss8 = sb1("ss8", [128, 16]); b_ss8 = Buf()
            glrT = sb1("glrT", [16, 128], BF16); b_glrT = Buf()
            ge = sb1("ge", [128, 2, 128]); b_ge = Buf()
            gl = sb1("gl", [128, 2, 128]); b_gl = Buf()
            gbl = sb1("gbl", [128, 2, 128]); b_gbl = Buf()
            nbl = sb1("nbl", [128, 2]); b_nbl = Buf()
            Eb = sb1("Eb", [128, 2, 128]); Enb = sb1("Enb", [128, 2, 128]); Ee = sb1("Ee", [128, 2, 128])
            b_Eb = Buf(); b_Enb = Buf(); b_Ee = Buf()
            zeros = sb1("zeros", [128, 128]); b_zeros = Buf()
            QdT = sb1("QdT", [128, 2, 2, 128], BF16); KdT = sb1("KdT", [128, 2, 128], BF16)
            KeT = sb1("KeT", [128, 2, 128], BF16); b_QdT = Buf(); b_KdT = Buf(); b_KeT = Buf()
            Ke = sb1("Ke", [128, 256], BF16); b_Ke = Buf()
            vsb = sb1("vsb", [128, 512], BF16); b_vsb = Buf()
            sr = sb1("sr", [128, 512]); b_sr = Buf()
            At = sb1("At", [128, 4, 128], BF16); b_At = Buf()
            S = sb1("S", [128, 2, 128]); Sbf = sb1("Sbf", [128, 2, 128], BF16); b_S = Buf(); b_Sbf = Buf()
            ss4 = sb1("ss4", [128, 8]); b_ss4 = Buf()
            rden = sb1("rden", [128, 8]); b_rden = Buf()
            mixs = [sb1("mix%d" % i, [128, D], BF16) for i in range(2)]; bmixs = [Buf(), Buf()]
            mixT = sb1("mixT", [128, 8, 128], BF16); b_mixT = Buf()

            K.op("pool", lambda h: h.memset(zeros[:], 0.0), writes=[b_zeros])
            K.op("pool", lambda h: h.memset(QdT[:], 0.0), writes=[b_QdT])
            for i in range(2):
                K.op("pool", lambda h, i=i: h.memset(QT[i][:], 0.0), writes=[b_QT[i]])
            K.op("pool", lambda h: h.memset(S[:], 0.0), writes=[b_S])
            K.op("pool", lambda h: h.memset(Sbf[:], 0.0), writes=[b_Sbf])

            def proj_tok(j, col0, pi, extra=()):
                hb = hT[j % PHT]
                for c in range(8):
                    K.op("pe", lambda h, c=c, hb=hb: h.matmul(
                        pP[pi][:, :], lhsT=hb[:, c, :], rhs=win[:, c, col0:col0 + 512],
                        start=(c == 0), stop=(c == 7)), reads=[b_hT[j % PHT], b_hTo[j % PHT]] + b_wins + list(extra), writes=[bP[pi]])

            def proj_feat(j, col0, dst, wbuf, width=128):
                hb = hT[j % PHT]
                for c in range(8):
                    K.op("pe", lambda h, c=c, hb=hb: h.matmul(
                        dst, lhsT=win[:, c, col0:col0 + width], rhs=hb[:, c, :],
                        start=(c == 0), stop=(c == 7)), reads=[b_hT[j % PHT], b_hTo[j % PHT]] + b_wins, writes=[wbuf])

            loaded_x = set()

            def load_x(j):
                if j in loaded_x or j >= NL:
                    return
                loaded_x.add(j)
                dma_in("sp", xt[j % 3][:], xl[j * 128:(j + 1) * 128, :], b_xt[j % 3])

            def front(j):
                mixb, bmix = mixs[j % 2], bmixs[j % 2]
                pGo = pP[0][:, :].rearrange("p (a b) -> p a b", a=4)
                kind = "FULL" if j >= HALO else ("GK" if j >= KV0 else "G")
                xb, bxb = xt[j % 3], b_xt[j % 3]
                xnb, bxn = xn[j % PXN], b_xn[j % PXN]
                hb, bhb = hT[j % PHT], b_hT[j % PHT]
                sl = j % KVRING
                va, bva = VA[sl], b_VA[sl]
                qb = QT[j % PQT]
                load_x(j)
                if kind != "FULL":
                    load_x(j + 1)
                    load_x(j + 2)

                def o_norm():
                    K.op("act", lambda h: h.activation(out=junk[:], in_=xb[:], func=AF.Square, accum_out=ss[:, 0:1]),
                         reads=[bxb], writes=[b_junk, b_ss])
                    K.op("act", lambda h: h.activation(out=ss[:, 2:3], in_=ss[:, 0:1], func=AF.Ln, scale=1.0 / D, bias=epsc[:, 0:1]),
                         reads=[b_ss], writes=[b_ss])
                    K.op("act", lambda h: h.activation(out=ss[:, 1:2], in_=ss[:, 2:3], func=AF.Exp, scale=-0.5),
                         reads=[b_ss], writes=[b_ss])

                def o_xn():
                    K.op("dve", lambda h: h.tensor_scalar(out=xnb[:], in0=xb[:], scalar1=ss[:, 1:2], scalar2=None, op0=ALU.mult),
                         reads=[bxb, b_ss], writes=[bxn])

                def o_xT():
                    for c in range(8):
                        K.op("pe", lambda h, c=c: h.transpose(pT[:, c * 128:(c + 1) * 128], xnb[:, c * 128:(c + 1) * 128], ident_sb[:]),
                             reads=[bxn, b_ident], writes=[bT])

                def o_hT():
                    for c in range(8):
                        K.op("dve", lambda h, c=c: h.tensor_scalar(
                            out=hb[:, c, :], in0=pT[:, c * 128:(c + 1) * 128], scalar1=G1[:, c:c + 1], scalar2=sh1[:, c:c + 1],
                            op0=ALU.mult, op1=ALU.add), reads=[bT, b_G, b_modT], writes=[bhb])

                def o_glr():
                    proj_feat(j, C_GLR, pG[0:16, 0:128], bG, width=16)

                def o_gq():
                    for q in range(2):
                        proj_feat(j, C_GQ + q * 128, pF[:, q, :], bF)

                def o_gk():
                    for q in range(2):
                        proj_feat(j, C_GK + q * 128, pF[:, 2 + q, :], bF)

                def o_glrT():
                    K.op("dve", lambda h: h.tensor_copy(out=glrT[:], in_=pG[0:16, 0:128]), reads=[bG], writes=[b_glrT])

                def o_z():
                    for f in range(2):
                        K.op("pe", lambda h, f=f: h.matmul(pG[:, 128 + f * 128:256 + f * 128], lhsT=wg_sb[0:16, f * 128:(f + 1) * 128],
                                                           rhs=glrT[0:16, :], start=True, stop=True),
                             reads=[b_glrT, b_wg], writes=[bG])

                def o_ge():
                    for f in range(2):
                        K.op("act", lambda h, f=f: h.activation(out=ge[:, f, :], in_=pG[:, 128 + f * 128:256 + f * 128], func=AF.Exp,
                                                                scale=-1.0, bias=nbg[:, f:f + 1]),
                             reads=[bG, b_small], writes=[b_ge])
                    K.op("act", lambda h: h.activation(out=gl[:], in_=ge[:], func=AF.Ln, bias=1.0, scale=1.0),
                         reads=[b_ge], writes=[b_gl])

                def o_scan():
                    for f in range(2):
                        K.op("dve", lambda h, f=f: h.tensor_tensor_scan(out=gbl[:, f, :], data0=zeros[:], data1=gl[:, f, :],
                                                                        initial=0.0, op0=ALU.add, op1=ALU.add),
                             reads=[b_gl, b_zeros], writes=[b_gbl])
                    K.op("dve", lambda h: h.tensor_scalar(out=nbl[:], in0=gbl[:, :, 127], scalar1=-1.0 / 16.0, scalar2=None,
                                                          op0=ALU.mult), reads=[b_gbl], writes=[b_nbl])

                def o_EbEe():
                    K.op("act", lambda h: h.activation(out=Eb[:], in_=gbl[:], func=AF.Exp, scale=-1.0 / 16.0),
                         reads=[b_gbl], writes=[b_Eb])
                    for f in range(2):
                        K.op("act", lambda h, f=f: h.activation(out=Ee[:, f, :], in_=gbl[:, f, :], func=AF.Exp, scale=1.0 / 16.0,
                                                                bias=nbl[:, f:f + 1]), reads=[b_gbl, b_nbl], writes=[b_Ee])

                def o_Enb():
                    K.op("act", lambda h: h.activation(out=Enb[:], in_=gbl[:], func=AF.Exp, scale=1.0 / 16.0),
                         reads=[b_gbl], writes=[b_Enb])

                def o_KeT():
                    K.op("dve", lambda h: h.tensor_tensor(out=KeT[:], in0=pF[:, 2:4, :], in1=Ee[:], op=ALU.mult),
                         reads=[bF, b_Ee], writes=[b_KeT])

                def o_QdKd():
                    for pos in range(2):
                        p0 = pos * 64
                        K.op("dve", lambda h, pos=pos, p0=p0: h.scalar_tensor_tensor(
                            out=QdT[p0:p0 + 64, pos, :, :], in0=pF[p0:p0 + 64, 0:2, :], scalar=0.125, in1=Eb[p0:p0 + 64, :, :],
                            op0=ALU.mult, op1=ALU.mult), reads=[bF, b_Eb], writes=[b_QdT])
                    K.op("dve", lambda h: h.tensor_tensor(out=KdT[:], in0=pF[:, 2:4, :], in1=Enb[:], op=ALU.mult),
                         reads=[bF, b_Enb], writes=[b_KdT])

                def o_KeTt():
                    for f in range(2):
                        K.op("pe", lambda h, f=f: h.transpose(pT[:, f * 128:(f + 1) * 128], KeT[:, f, :], ident_sb[:]),
                             reads=[b_KeT, b_ident], writes=[bT])

                def o_Ke():
                    K.op("dve", lambda h: h.tensor_copy(out=Ke[:], in_=pT[:, 0:256]), reads=[bT], writes=[b_Ke])

                def o_v():
                    proj_tok(j, C_GV, 0)

                def o_vsb():
                    K.op("dve", lambda h: h.tensor_copy(out=vsb[:], in_=pP[0][:, :]), reads=[bP[0]], writes=[b_vsb])

                def o_r():
                    proj_tok(j, C_GR, 1)

                def o_rexp():
                    K.op("act", lambda h: h.activation(out=sr[:], in_=pP[1][:, :], func=AF.Exp, scale=-1.0), reads=[bP[1]], writes=[b_sr])

                def o_rln():
                    K.op("act", lambda h: h.activation(out=sr[:], in_=sr[:], func=AF.Ln, bias=1.0, scale=1.0), reads=[b_sr], writes=[b_sr])
                    K.op("act", lambda h: h.activation(out=sr[:], in_=sr[:], func=AF.Exp, scale=-1.0), reads=[b_sr], writes=[b_sr])

                def o_silu():
                    K.op("dve", lambda h: h.tensor_tensor(out=sr[:], in0=pP[1][:, :], in1=sr[:], op=ALU.mult),
                         reads=[bP[1], b_sr], writes=[b_sr])
                    K.op("pool", lambda h: h.tensor_tensor(out=sr[:], in0=sr[:], in1=gg[:], op=ALU.mult),
                         reads=[b_sr, b_small], writes=[b_sr])

                def o_AT():
                    for hh in range(4):
                        ch = hh // 2
                        K.op("pe", lambda h, hh=hh, ch=ch: h.matmul(
                            pF[:, hh, :], lhsT=KdT[:, ch, :], rhs=QdT[:, hh % 2, ch, :], start=True, stop=True),
                            reads=[b_KdT, b_QdT, b_KeT], writes=[bF])

                def o_At():
                    K.op("dve", lambda h: h.tensor_tensor(out=At[:], in0=pF[:], in1=tri_sb[:], op=ALU.mult),
                         reads=[bF, b_cst], writes=[b_At])

                def o_o():
                    for hh in range(4):
                        ch = hh // 2
                        K.op("pe", lambda h, hh=hh, ch=ch: h.matmul(
                            pGo[:, hh, :], lhsT=QdT[:, hh % 2, ch, :], rhs=Sbf[:, ch, :], start=True, stop=False),
                            reads=[b_QdT, b_Sbf], writes=[bP[0]])
                        K.op("pe", lambda h, hh=hh: h.matmul(
                            pGo[:, hh, :], lhsT=At[:, hh, :], rhs=vsb[:, hh * 128:(hh + 1) * 128], start=False, stop=True),
                            reads=[b_At, b_vsb], writes=[bP[0]])

                def o_osq():
                    K.op("act", lambda h: h.activation(out=sq[:], in_=pP[0][:, :], func=AF.Square),
                         reads=[bP[0]], writes=[b_sq])

                def o_ored():
                    K.op("dve", lambda h: h.tensor_reduce(out=ss4[:, 0:4], in_=sq[:].rearrange("p (a b) -> p a b", a=4),
                                                          axis=AX.X, op=ALU.add), reads=[b_sq], writes=[b_ss4])
                    K.op("act", lambda h: h.activation(out=ss4[:, 0:4], in_=ss4[:, 0:4], func=AF.Ln, scale=1.0 / 128, bias=epsc[:, 0:1]),
                         reads=[b_ss4], writes=[b_ss4])
                    K.op("act", lambda h: h.activation(out=ss4[:, 4:8], in_=ss4[:, 0:4], func=AF.Exp, scale=-0.5),
                         reads=[b_ss4], writes=[b_ss4])

                def o_mix():
                    for hh in range(4):
                        K.op("dve", lambda h, hh=hh: h.scalar_tensor_tensor(
                            out=mixb[:, hh * 128:(hh + 1) * 128], in0=pGo[:, hh, :], scalar=ss4[:, 4 + hh:5 + hh],
                            in1=sr[:, hh * 128:(hh + 1) * 128], op0=ALU.mult, op1=ALU.mult),
                            reads=[bP[0], b_ss4, b_sr], writes=[bmix])

                def o_st():
                    for hh in range(4):
                        ch = hh // 2
                        K.op("pe", lambda h, hh=hh, ch=ch: h.matmul(
                            pG[:, hh * 128:(hh + 1) * 128], lhsT=Ke[:, ch * 128:(ch + 1) * 128],
                            rhs=vsb[:, hh * 128:(hh + 1) * 128], start=True, stop=True),
                            reads=[b_Ke, b_vsb], writes=[bG])

                def o_S():
                    for hh in range(4):
                        pr = (hh % 2) * 64; ch = hh // 2
                        K.op("dve", lambda h, hh=hh, pr=pr, ch=ch: h.scalar_tensor_tensor(
                            out=S[pr:pr + 64, ch, :], in0=S[pr:pr + 64, ch, :], scalar=Eb[pr:pr + 64, ch, 127:128],
                            in1=pG[pr:pr + 64, hh * 128:(hh + 1) * 128],
                            op0=ALU.mult, op1=ALU.add), reads=[b_S, b_Eb, bG], writes=[b_S])
                    if j == HALO:
                        K.op("dve", lambda h: h.tensor_scalar(out=S[:], in0=S[:], scalar1=flag_sb[:, 0:1], scalar2=None, op0=ALU.mult),
                             reads=[b_S, b_flag], writes=[b_S])
                    K.op("pool", lambda h: h.tensor_copy(out=Sbf[:], in_=S[:]), reads=[b_S], writes=[b_Sbf])

                def mk_qk(nm, col0, o8):
                    def o_p():
                        proj_tok(j, col0, 1)

                    def o_sq():
                        K.op("act", lambda h: h.activation(out=sq[:], in_=pP[1][:, :], func=AF.Square), reads=[bP[1]], writes=[b_sq])

                    def o_red():
                        K.op("dve", lambda h: h.tensor_reduce(out=ss8[:, o8:o8 + 8], in_=sq[:].rearrange("p (a b) -> p a b", a=8),
                                                              axis=AX.X, op=ALU.add), reads=[b_sq], writes=[b_ss8])
                        K.op("act", lambda h: h.activation(out=ss8[:, o8:o8 + 8], in_=ss8[:, o8:o8 + 8], func=AF.Ln, scale=1.0 / 64,
                                                           bias=epsc[:, 0:1]), reads=[b_ss8], writes=[b_ss8])
                        K.op("act", lambda h: h.activation(out=ss8[:, o8:o8 + 8], in_=ss8[:, o8:o8 + 8], func=AF.Exp, scale=-0.5),
                             reads=[b_ss8], writes=[b_ss8])

                    def o_n():
                        K.op("dve", lambda h: h.tensor_tensor(
                            out=qkn[:].rearrange("p (a b) -> p a b", a=8), in0=pP[1][:, :].rearrange("p (a b) -> p a b", a=8),
                            in1=ss8[:, o8:o8 + 8].rearrange("p (a o) -> p a o", o=1).broadcast_to([128, 8, 64]), op=ALU.mult),
                            reads=[bP[1], b_ss8], writes=[b_qkn])

                    def o_T():
                        for c in range(4):
                            K.op("pe", lambda h, c=c: h.transpose(pT[:, c * 128:(c + 1) * 128], qkn[:, c * 128:(c + 1) * 128], ident_sb[:]),
                                 reads=[b_qkn, b_ident], writes=[bT])

                    def o_ev():
                        if nm == "k":
                            K.op("dve", lambda h: h.tensor_scalar(out=KT[sl][:].rearrange("p a b -> p (a b)"), in0=pT[:, 0:512],
                                                                  scalar1=gk_sb[:, 0:1], scalar2=None, op0=ALU.mult),
                                 reads=[bT, b_small], writes=[b_KT[sl]])
                        else:
                            for pos in range(2):
                                p0 = pos * 64
                                K.op("dve", lambda h, pos=pos, p0=p0: h.tensor_scalar(
                                    out=qb[p0:p0 + 64, pos, :, :], in0=pT[p0:p0 + 64, 0:512].rearrange("p (a b) -> p a b", a=4),
                                    scalar1=gq_sb[p0:p0 + 64, 0:1], scalar2=0.125, op0=ALU.mult, op1=ALU.mult),
                                    reads=[bT, b_small], writes=[b_QT[j % PQT]])
                    return o_p, o_sq, o_red, o_n, o_T, o_ev

                o_k, o_ksq, o_kred, o_kn, o_kT, o_KT = mk_qk("k", C_AK, 0)
                o_q, o_qsq, o_qred, o_qn, o_qT, o_QT = mk_qk("q", C_AQ, 8)

                def o_av():
                    proj_tok(j, C_AV, 0)

                def o_VA():
                    K.op("dve", lambda h: h.tensor_copy(out=va[:, :, 0:64], in_=pP[0][:, :].rearrange("p (a b) -> p a b", a=8)),
                         reads=[bP[0]], writes=[bva])
                    K.op("pool", lambda h: h.memset(va[:, :, 64:65], 1.0), writes=[bva])
                    if j < NM:
                        K.op("dve", lambda h: h.tensor_scalar(out=va[:], in0=va[:], scalar1=flag_sb[:, 0:1], scalar2=None,
                                                              op0=ALU.mult), reads=[bva, b_flag], writes=[bva])

                def o_warm():
                    if kind == "FULL":
                        return
                    for _ in range(3):
                        K.op("pe", lambda h: h.matmul(pS[0][:, :], lhsT=ident_sb[:], rhs=dn_sb[:, 0:4, :].rearrange("p a b -> p (a b)"),
                                                      start=True, stop=True), reads=[b_ident, b_cst], writes=[bS[0]])

                G_OPS = {o_warm, o_norm, o_xn, o_xT, o_hT, o_glr, o_gk, o_v, o_glrT, o_vsb, o_z, o_ge, o_scan, o_EbEe, o_KeT, o_KeTt, o_Ke, o_st, o_S}
                GK_OPS = G_OPS | {o_k, o_ksq, o_kred, o_kn, o_kT, o_KT, o_av, o_VA}
                sched = [
                    [o_norm], [], [o_xn], [], [o_xT], [], [o_hT], "A", [],
                    [o_glr, o_gk, o_gq], [o_v], [o_glrT, o_vsb], [o_r, o_z], [o_ge, o_rexp, o_warm], [o_scan, o_rln, o_warm],
                    [o_EbEe, o_Enb, o_warm], [o_KeT, o_QdKd, o_silu, o_warm], [o_KeTt, o_k], [o_Ke, o_AT, o_ksq], [o_At, o_kred],
                    [o_o, o_st], [o_osq, o_kn, o_S], [o_ored, o_kT], [o_q, o_KT], [o_mix, o_qsq], [o_av, o_qred],
                    [o_VA, o_qn], [o_qT], [o_QT],
                ]
                allowed = None if kind == "FULL" else (GK_OPS if kind == "GK" else G_OPS)
                for step in sched:
                    if step == "A":
                        yield "A"
                        continue
                    did = False
                    for f in step:
                        if allowed is None or f in allowed:
                            f(); did = True
                    if did or kind == "FULL":
                        yield None

            def attn(j, filler):
                mixb, bmix = mixs[j % 2], bmixs[j % 2]
                xb, bxb = xt[j % 3], b_xt[j % 3]
                qb, bqb = QT[j % PQT], b_QT[j % PQT]
                groups = []
                for hh in range(8):
                    ents = [(i, g, m) for i, (g, m) in enumerate(ENTRIES) if j - m >= KV0]
                    k0 = 0
                    while k0 < len(ents):
                        grp = ents[k0:k0 + 4]
                        groups.append((hh, grp, k0 == 0, k0 + 4 >= len(ents)))
                        k0 += 4
                state = {"n": 0}

                def issue_qk(gi):
                    hh, grp, first, last = groups[gi]
                    pr = (hh % 2) * 64; ch = hh // 2
                    sbk = gi % 2
                    contiguous = all(grp[t][0] == grp[0][0] + t for t in range(len(grp)))
                    n = len(grp)
                    if contiguous:
                        i0 = grp[0][0]
                        K.op("pe", lambda h, hh=hh, i0=i0, n=n, sbk=sbk: h.matmul(
                            pS[sbk][:, 0:n * 128], lhsT=nsI_sb[:, hh, :], rhs=dn_sb[:, i0:i0 + n, :].rearrange("p a b -> p (a b)"),
                            start=True, stop=False), reads=[b_cst], writes=[bS[sbk]])
                    else:
                        for t, (i, g, m) in enumerate(grp):
                            K.op("pe", lambda h, hh=hh, i=i, t=t, sbk=sbk: h.matmul(
                                pS[sbk][:, t * 128:(t + 1) * 128], lhsT=nsI_sb[:, hh, :], rhs=dn_sb[:, i, :],
                                start=(t == 0), stop=False), reads=[b_cst], writes=[bS[sbk]])
                    for t, (i, g, m) in enumerate(grp):
                        ks = (j - m) % KVRING
                        K.op("pe", lambda h, t=t, ks=ks, hh=hh, ch=ch, sbk=sbk, n=n, qb=qb: h.matmul(
                            pS[sbk][:, t * 128:(t + 1) * 128], lhsT=KT[ks][:, ch, :], rhs=qb[:, hh % 2, ch, :],
                            start=False, stop=(t == n - 1)), reads=[b_KT[ks], bqb], writes=[bS[sbk]])
                    pb = gi % 3
                    K.op("act", lambda h, sbk=sbk, pb=pb, n=n: h.activation(out=Pt[pb][:, 0:n * 128], in_=pS[sbk][:, 0:n * 128], func=AF.Exp),
                         reads=[bS[sbk]], writes=[b_Pt[pb]])

                def issue_pv(gi):
                    hh, grp, first, last = groups[gi]
                    pb = gi % 3
                    n = len(grp)
                    for t, (i, g, m) in enumerate(grp):
                        ks = (j - m) % KVRING
                        K.op("pe", lambda h, hh=hh, t=t, ks=ks, pb=pb, st=(first and t == 0), sp=(last and t == n - 1): h.matmul(
                            pO[:, hh % 4, 0:65], lhsT=Pt[pb][:, t * 128:(t + 1) * 128], rhs=VA[ks][:, hh, :],
                            start=st, stop=sp), reads=[b_Pt[pb], b_VA[ks]], writes=[bO])

                def finish_half(half):
                    K.op("dve", lambda h, half=half: h.tensor_scalar(out=rden[:, half * 4:half * 4 + 4], in0=pO[:, :, 64], scalar1=1e-30,
                                                                     scalar2=None, op0=ALU.add), reads=[bO], writes=[b_rden])
                    K.op("dve", lambda h, half=half: h.reciprocal(out=rden[:, half * 4:half * 4 + 4], in_=rden[:, half * 4:half * 4 + 4]),
                         reads=[b_rden], writes=[b_rden])
                    for t in range(4):
                        hh = half * 4 + t
                        K.op("dve", lambda h, hh=hh, t=t: h.tensor_scalar(
                            out=mixb[:, 512 + hh * 64:512 + (hh + 1) * 64], in0=pO[:, t, 0:64], scalar1=rden[:, hh:hh + 1],
                            scalar2=None, op0=ALU.mult), reads=[bO, b_rden], writes=[bmix])

                ng = len(groups)
                LAG = 2
                for gi in range(ng + LAG):
                    if gi < ng:
                        issue_qk(gi)
                    if gi >= LAG:
                        issue_pv(gi - LAG)
                        hh_prev, _, _, last_prev = groups[gi - LAG]
                        if last_prev and hh_prev % 4 == 3:
                            finish_half(hh_prev // 4)
                    if filler is not None and gi >= 1:
                        next(filler, None)

                if filler is not None:
                    for _ in filler:
                        pass

            def tail(j):
                mixb, bmix = mixs[j % 2], bmixs[j % 2]
                xb, bxb = xt[j % 3], b_xt[j % 3]
                for c in range(8):
                    K.op("pe", lambda h, c=c: h.transpose(pT[:, c * 128:(c + 1) * 128], mixb[:, c * 128:(c + 1) * 128], ident_sb[:]),
                         reads=[bmix, b_ident], writes=[bT])
                yield
                yield
                K.op("dve", lambda h: h.tensor_copy(out=mixT[:].rearrange("p a b -> p (a b)"), in_=pT[:, :]), reads=[bT], writes=[b_mixT])
                yield
                yield
                x1b, bx1 = xb, bxb
                for hf in range(2):
                    for c in range(8):
                        K.op("pe", lambda h, c=c, hf=hf: h.matmul(
                            pP[hf][:, :], lhsT=mixT[:, c, :], rhs=wout[:, c, hf * 512:(hf + 1) * 512],
                            start=(c == 0), stop=(c == 7)), reads=[b_mixT, b_wout], writes=[bP[hf]])
                    K.op("dve", lambda h, hf=hf, x1b=x1b, xb=xb: h.tensor_tensor(
                        out=x1b[:, hf * 512:(hf + 1) * 512], in0=pP[hf][:, :], in1=xb[:, hf * 512:(hf + 1) * 512], op=ALU.add),
                        reads=[bP[hf], bxb], writes=[bx1])
                    yield
                qi = j - HALO
                K.dma("sp", lambda h, x1b=x1b, qi=qi: h.dma_start(out=x1d[qi * 128:(qi + 1) * 128, :], in_=x1b[:]),
                      reads=[bx1], writes=[])
                load_x(j + 3)

            def run_all(gen):
                if gen is not None:
                    for _ in gen:
                        pass

            import itertools
            pending = None
            pending_tail = None
            pre = [j for j in range(NL if STOP >= 1 else 0) if j < HALO]
            def adv_A(gen):
                for v in gen:
                    if v == "A":
                        break

            cur = None
            if pre:
                cur = front(pre[0]); adv_A(cur)
            for idx, j in enumerate(pre):
                nxt = None
                if idx + 1 < len(pre):
                    nxt = front(pre[idx + 1]); adv_A(nxt)
                run_all(cur)
                cur = nxt
            for j in range(NL if STOP >= 1 else 0):
                kind_j = "FULL" if j >= HALO else ("GK" if j >= KV0 else "G")
                if kind_j != "FULL":
                    continue
                if j == HALO:
                    run_all(front(j))
                    load_x(j + 1)
                    load_x(j + 2)
                else:
                    run_all(pending)
                pending = front(j + 1) if j + 1 < NL else None
                attn(j, itertools.chain(*[g for g in (pending_tail, pending) if g is not None]))
                pending_tail = tail(j)
            run_all(pending_tail)
            K.barrier()

        with ExitStack() as e2:
            def sb2(name, shape, dt=F32):
                return e2.enter_context(nc.sbuf_tensor(name, list(shape), dt))
            GT = 384
            NT = GT // 128
            wup = sb2("wup", [128, 8, 2 * DFF], BF16); b_wups = [Buf() for _ in range(8)]
            wdn = sb2("wdn", [128, NGC, D], BF16); b_wdn = Buf()
            cw = sb2("cw", [128, 3, NFC]); cb = sb2("cb", [128, NFC]); b_cw = Buf()
            dma_in("sp", cw[:], cwT, b_cw); dma_in("sp", cb[:], cbT, b_cw)
            w_up_v = w_up.rearrange("(c p) n -> p c n", p=128)
            for i in (0, 4, 1, 5, 2, 6, 3, 7):
                c0 = i * 704
                K.dma("pool", lambda h, c0=c0: h.dma_start(out=wup[:, :, c0:c0 + 704], in_=w_up_v[:, :, c0:c0 + 704]),
                      writes=[b_wups[i]])
            pU = [pP[0], pP[1], pS[0], pS[1]]; bU = [bP[0], bP[1], bS[0], bS[1]]
            pD = [pF, pO]; bD = [bF, bO]
            x1a = [sb2("x1a%d" % i, [128, D]) for i in range(3)]; b_x1a = [Buf(), Buf(), Buf()]
            x1r = [sb2("x1r%d" % i, [128, D]) for i in range(2)]; b_x1r = [Buf(), Buf()]
            stg2 = x1r; b_stg2 = b_x1r

            def wdn_loader():
                for i in range(NGC):
                    s_ = i % 2
                    dma_in("sp", stg2[s_][:], w_down[i * 128:(i + 1) * 128, :], b_stg2[s_])
                    if i >= 1:
                        yield
                    K.op("dve", lambda h, s_=s_, i=i: h.tensor_tensor(out=wdn[:, i, :], in0=stg2[s_][:], in1=g2b[:], op=ALU.mult),
                         reads=[b_stg2[s_], b_gb], writes=[b_wdn])
                    yield
            wdn_gen = wdn_loader()
            ssb = sb2("ssb", [128, 12]); b_ssbs = [Buf(), Buf(), Buf()]
            xn2 = [sb2("xn2_%d" % i, [128, D], BF16) for i in range(3)]; b_xn2 = [Buf(), Buf(), Buf()]
            h2T = [sb2("h2T%d" % i, [128, 8, GT], BF16) for i in range(1)] * 2; b_h2T = [Buf()] * 2
            NU = 4
            usb = [sb2("usb%d" % i, [128, GT + 2]) for i in range(NU)]; b_usb = [Buf() for _ in range(NU)]; b_ush = [Buf() for _ in range(NU)]
            NY = 5
            ysb = [sb2("ysb%d" % i, [128, GT]) for i in range(NY)]; b_ysb = [Buf() for _ in range(NY)]
            sg = [sb2("sg%d" % i, [128, GT]) for i in range(2)]; b_sg = [Buf(), Buf()]
            mT = [sb2("mT%d" % i, [128, NGC, GT], BF16) for i in range(1)]; b_mT = [Buf()]
            hal = sb2("hal", [128, NFC, 2]); b_hal = [Buf() for _ in range(NFC)]
            K.op("pool", lambda h: h.memset(hal[:], 0.0), writes=b_hal)

            groups2 = [[0]] + [list(range(q0, min(q0 + NT, NM + 1))) for q0 in range(1, NM + 1, NT)]
            if STOP < 2:
                groups2 = []
            tcnt = {"n": 0}

            loaded_g = set()

            def load_x1(gidx):
                if gidx in loaded_g or gidx >= len(groups2):
                    return
                loaded_g.add(gidx)
                for qi in groups2[gidx]:
                    K.dma("sp", lambda h, qi=qi: h.dma_start(out=x1a[qi % 3][:], in_=x1d[qi * 128:(qi + 1) * 128, :]),
                          writes=[b_x1a[qi % 3]])

            def front2_a(gidx):
                tiles = groups2[gidx]
                load_x1(gidx)
                for t, qi in enumerate(tiles):
                    xa, bxa = x1a[qi % 3], b_x1a[qi % 3]
                    xnb, bxn = xn2[qi % 3], b_xn2[qi % 3]
                    o4 = (qi % 3) * 4
                    b_ssb = b_ssbs[qi % 3]
                    K.op("act", lambda h, xa=xa, xnb=xnb, o4=o4: h.activation(out=xnb[:], in_=xa[:], func=AF.Square, accum_out=ssb[:, o4:o4 + 1]),
                         reads=[bxa], writes=[bxn, b_ssb])
                    K.op("act", lambda h, o4=o4: h.activation(out=ssb[:, o4 + 2:o4 + 3], in_=ssb[:, o4:o4 + 1], func=AF.Ln, scale=1.0 / D, bias=epsc[:, 0:1]),
                         reads=[b_ssb], writes=[b_ssb])
                    K.op("act", lambda h, o4=o4: h.activation(out=ssb[:, o4 + 1:o4 + 2], in_=ssb[:, o4 + 2:o4 + 3], func=AF.Exp, scale=-0.5),
                         reads=[b_ssb], writes=[b_ssb])
                    K.op("dve", lambda h, xa=xa, xnb=xnb, o4=o4: h.tensor_scalar(out=xnb[:], in0=xa[:], scalar1=ssb[:, o4 + 1:o4 + 2], scalar2=None,
                                                                                 op0=ALU.mult), reads=[bxa, b_ssb], writes=[bxn])

            def front2_b(gidx, only_t=None):
                if gidx >= len(groups2):
                    return
                tiles = groups2[gidx]
                hb, bhb = h2T[gidx % 2], b_h2T[gidx % 2]
                for t, qi in enumerate(tiles):
                    if only_t is not None and t != only_t:
                        continue
                    xnb, bxn = xn2[qi % 3], b_xn2[qi % 3]
                    for c in range(8):
                        K.op("pe", lambda h, c=c, xnb=xnb: h.transpose(pT[:, c * 128:(c + 1) * 128], xnb[:, c * 128:(c + 1) * 128], ident_sb[:]),
                             reads=[bxn, b_ident], writes=[bT])
                    for c in range(8):
                        K.op("act", lambda h, c=c, hb=hb, t=t: h.activation(
                            out=hb[:, c, t * 128:(t + 1) * 128], in_=pT[:, c * 128:(c + 1) * 128], func=AF.Identity,
                            scale=G2[:, c:c + 1], bias=sh2[:, c:c + 1]), reads=[bT, b_G, b_modT], writes=[bhb])

            def upconv(gidx):
                tiles = groups2[gidx]
                is_halo = (gidx == 0)
                ntok = 128 * len(tiles)
                hb, bhb = h2T[gidx % 2], b_h2T[gidx % 2]
                mb, bmb = mT[0], b_mT[0]
                load_x1(gidx + 1)
                order = []
                for i in range(NGC):
                    order += [i, NGC + i]

                def stA(oi):
                    fc = order[oi]
                    ub, bub = pU[oi % 4], bU[oi % 4]
                    us, bus, bush = usb[oi % NU], b_usb[oi % NU], b_ush[oi % NU]
                    for c in range(8):
                        K.op("pe", lambda h, c=c, fc=fc, ub=ub: h.matmul(
                            ub[:, 0:ntok], lhsT=wup[:, c, fc * 128:(fc + 1) * 128], rhs=hb[:, c, 0:ntok],
                            start=(c == 0), stop=(c == 7)), reads=[bhb] + [b_wups[c_] for c_ in range((fc * 128) // 704, (fc * 128 + 127) // 704 + 1)], writes=[bub])
                    K.op("pool", lambda h, fc=fc, us=us: h.tensor_copy(out=us[:, 0:2], in_=hal[:, fc, :]),
                         reads=[b_hal[fc]], writes=[bush])
                    K.op("act", lambda h, ub=ub, us=us: h.copy(out=us[:, 2:2 + ntok], in_=ub[:, 0:ntok]),
                         reads=[bub], writes=[bus])
                    K.op("pool", lambda h, fc=fc, us=us: h.tensor_copy(out=hal[:, fc, :], in_=us[:, ntok:ntok + 2]),
                         reads=[bus], writes=[b_hal[fc]])

                def stB(oi):
                    fc = order[oi]
                    us, bus, bush = usb[oi % NU], b_usb[oi % NU], b_ush[oi % NU]
                    yb, byb = ysb[oi % NY], b_ysb[oi % NY]
                    K.op("act", lambda h, fc=fc, us=us, yb=yb: h.activation(
                        out=yb[:, 0:ntok], in_=us[:, 2:2 + ntok], func=AF.Identity, scale=cw[:, 2, fc:fc + 1], bias=cb[:, fc:fc + 1]),
                        reads=[bus, b_cw], writes=[byb])
                    K.op("dve", lambda h, fc=fc, us=us, yb=yb: h.scalar_tensor_tensor(
                        out=yb[:, 0:ntok], in0=us[:, 1:1 + ntok], scalar=cw[:, 1, fc:fc + 1], in1=yb[:, 0:ntok],
                        op0=ALU.mult, op1=ALU.add), reads=[bus, bush, b_cw, byb], writes=[byb])
                    K.op("dve", lambda h, fc=fc, us=us, yb=yb: h.scalar_tensor_tensor(
                        out=yb[:, 0:ntok], in0=us[:, 0:ntok], scalar=cw[:, 0, fc:fc + 1], in1=yb[:, 0:ntok],
                        op0=ALU.mult, op1=ALU.add), reads=[bus, bush, b_cw, byb], writes=[byb])

                def stC1(i):
                    yg, byg = ysb[(2 * i) % NY], b_ysb[(2 * i) % NY]
                    sgb, bsg = sg[i % 2], b_sg[i % 2]
                    K.op("act", lambda h, yg=yg, sgb=sgb: h.activation(out=sgb[:, 0:ntok], in_=yg[:, 0:ntok], func=AF.Silu),
                         reads=[byg], writes=[bsg])

                def stC2(i):
                    yv, byv = ysb[(2 * i + 1) % NY], b_ysb[(2 * i + 1) % NY]
                    sgb, bsg = sg[i % 2], b_sg[i % 2]
                    K.op("pool", lambda h, i=i, sgb=sgb, yv=yv: h.tensor_tensor(
                        out=mb[:, i, 0:ntok], in0=sgb[:, 0:ntok], in1=yv[:, 0:ntok], op=ALU.mult),
                        reads=[bsg, byv], writes=[bmb])

                n = len(order)
                for oi in range(n + 5):
                    if is_halo:
                        next(wdn_gen, None)
                    if oi == 20 and gidx + 1 < len(groups2):
                        front2_a(gidx + 1)
                    if oi >= n and (oi - n) % 2 == 0:
                        front2_b(gidx + 1, only_t=(oi - n) // 2)
                    if oi < n:
                        stA(oi)
                    if (not is_halo) and 0 <= oi - 1 < n:
                        stB(oi - 1)
                    if (not is_halo) and 0 <= oi - 3 < n and (oi - 3) % 2 == 0:
                        stC1((oi - 3) // 2)
                    if (not is_halo) and 0 <= oi - 5 < n and (oi - 5) % 2 == 0:
                        stC2((oi - 5) // 2)
                if is_halo:
                    K.op("dve", lambda h: h.tensor_scalar(out=hal[:], in0=hal[:], scalar1=flag_sb[:, 0:1], scalar2=None, op0=ALU.mult),
                         reads=b_hal + [b_flag], writes=b_hal)

            def down(gidx):
                tiles = groups2[gidx]
                mb, bmb = mT[0], b_mT[0]
                for t, qi in enumerate(tiles):
                    xr, bxr = x1r[qi % 2], b_x1r[qi % 2]
                    K.dma("sp", lambda h, xr=xr, qi=qi: h.dma_start(out=xr[:], in_=x1d[qi * 128:(qi + 1) * 128, :]), writes=[bxr])
                    for hf in range(2):
                        db, bdb = pD[hf], bD[hf]
                        dflat = db[:].rearrange("p a b -> p (a b)")
                        for i in range(NGC):
                            K.op("pe", lambda h, i=i, hf=hf, dflat=dflat, t=t: h.matmul(
                                dflat, lhsT=mb[:, i, t * 128:(t + 1) * 128], rhs=wdn[:, i, hf * 512:(hf + 1) * 512],
                                start=(i == 0), stop=(i == NGC - 1)), reads=[bmb, b_wdn], writes=[bdb])
                        K.op("dve", lambda h, hf=hf, dflat=dflat, xr=xr: h.tensor_tensor(
                            out=xr[:, hf * 512:(hf + 1) * 512], in0=dflat, in1=xr[:, hf * 512:(hf + 1) * 512], op=ALU.add),
                            reads=[bdb, bxr], writes=[bxr])
                    oi_ = qi - 1
                    K.dma("sp", lambda h, xr=xr, oi_=oi_: h.dma_start(out=out[oi_ * 128:(oi_ + 1) * 128, :], in_=xr[:]),
                          reads=[bxr], writes=[])

            if groups2:
                front2_a(0)
                front2_b(0)
            for gidx in range(len(groups2)):
                upconv(gidx)
                if gidx == 0:
                    for _ in wdn_gen:
                        pass
                if gidx > 0:
                    down(gidx)
            K.barrier()

        with nc.Block() as block:
            @block.tensor
            def _(e):
                K.emit("pe", e)

            @block.scalar
            def _(e):
                K.emit("act", e)

            @block.vector
            def _(e):
                K.emit("dve", e)

            @block.gpsimd
            def _(e):
                K.emit("pool", e)

            @block.sync
            def _(e):
                K.emit("sp", e)
    return nc


def _consts():
    bf = ml_dtypes.bfloat16
    ident = np.eye(128, dtype=np.float32).astype(bf)
    slopes = np.array([2.0 ** (-8.0 * (h + 1) / 8) for h in range(8)], dtype=np.float32)
    nsI = np.zeros((128, 8, 128), np.float32)
    for h in range(8):
        nsI[np.arange(128), h, np.arange(128)] = -slopes[h]
    k = np.arange(128)[:, None]
    q = np.arange(128)[None, :]
    dnm = np.zeros((128, 24, 128), np.float32)
    for i, (g, m) in enumerate(ENTRIES):
        d = DILS[g]
        dist = 128 * m + q - k
        valid = (dist >= 0) & (dist % d == 0) & (dist <= 128 * d)
        dnm[:, i, :] = np.where(valid, dist, BIGD)
    tri = (k <= q).astype(np.float32)
    tri4 = np.repeat(tri[:, None, :], 4, axis=1)
    return ident, nsI.astype(bf), dnm.astype(bf), np.ascontiguousarray(tri4)


def make_in_maps(inputs, NM):
    S_half = NM * 128
    x = np.asarray(inputs["x"], np.float32)
    B = x.shape[0]
    ident, nsI, dnm, tri4 = _consts()
    f = lambda a: np.ascontiguousarray(np.asarray(a, np.float32))
    shared = {
        "w_ada": f(inputs["w_ada"][0]), "b_ada": f(inputs["b_ada"][0]).reshape(1, -1),
        "g1row": f(inputs["norm1_g"][0]).reshape(1, -1), "g2row": f(inputs["norm2_g"][0]).reshape(1, -1),
        "w_in": f(inputs["w_in"][0]), "wg": f(inputs["gla_w_gate"][0]),
        "nbgT": f(-np.asarray(inputs["gla_b_gate"][0], np.float32).reshape(2, 128).T),
        "ggb": f(np.tile(np.asarray(inputs["gla_norm_g"][0], np.float32)[None, :], (128, 4))),
        "gqc": f(np.tile(np.asarray(inputs["q_norm_g"][0], np.float32), 2).reshape(128, 1)),
        "gkc": f(np.tile(np.asarray(inputs["k_norm_g"][0], np.float32), 2).reshape(128, 1)),
        "w_out": f(inputs["w_out"][0]), "w_up": f(inputs["w_up"][0]),
        "cwT": f(np.asarray(inputs["conv_w"][0], np.float32).reshape(3, NFC, 128).transpose(2, 0, 1)),
        "cbT": f(np.asarray(inputs["conv_b"][0], np.float32).reshape(NFC, 128).T),
        "w_down": f(inputs["w_down"][0]),
        "ident": ident, "nsI": nsI, "dn": dnm, "tri4": tri4,
    }
    maps = []
    for core in range(2 * B):
        b, half = core // 2, core % 2
        if half == 0:
            xloc = np.concatenate([np.zeros((S_half, D), np.float32), x[b, :S_half]], axis=0)
        else:
            xloc = x[b, :2 * S_half]
        m = dict(shared)
        m["xl"] = np.ascontiguousarray(xloc)
        m["cT"] = f(np.asarray(inputs["c"], np.float32)[b].reshape(128, 8))
        m["flag"] = np.full((128, 1), float(half), np.float32)
        maps.append(m)
    return maps


_NC_CACHE = {}


def run(inputs, NM, debug=False):
    key = (NM, debug)
    if key not in _NC_CACHE:
        _NC_CACHE[key] = build(NM, debug)
    nc = _NC_CACHE[key]
    maps = make_in_maps(inputs, NM)
    res = run_bass_kernel_spmd(nc, maps, core_ids=list(range(len(maps))))
    return res


def kernel(**inputs):
    NM = 32
    res = run(inputs, NM)
    B = np.asarray(inputs["x"]).shape[0]
    outp = np.zeros((B, 2 * NM * 128, D), np.float32)
    for core in range(2 * B):
        b, half = core // 2, core % 2
        outp[b, half * NM * 128:(half + 1) * NM * 128] = np.asarray(res.results[core]["out"], np.float32)
    return outp
```

```python
import os
from contextlib import ExitStack
import numpy as np
import ml_dtypes
import concourse.bass as bass
import concourse.mybir as mybir
from concourse.bass_utils import run_bass_kernel_spmd

F32 = mybir.dt.float32
BF16 = mybir.dt.bfloat16
ALU = mybir.AluOpType
AF = mybir.ActivationFunctionType
AX = mybir.AxisListType

D = 1024
NCH = 8
IN_W = 3088
DFF = 2816
NFC = 44
NGC = 22
EPS = 1e-6
C_GQ, C_GK, C_GV, C_GR, C_GLR, C_AQ, C_AK, C_AV = 0, 256, 512, 1024, 1536, 1552, 2064, 2576
DILS = (1, 4, 16)
BIGD = float(2 ** 18)
ENTRIES = [(g, m) for g in range(3) for m in range(DILS[g] + 1)]
KVRING = 18


class Buf:
    __slots__ = ("w", "r", "name")

    def __init__(self, name=""):
        self.w = None
        self.r = {}
        self.name = name


class Sched:
    def __init__(self, nc, sems, dma_sems):
        self.nc = nc
        self.engs = {}
        for n in ("pe", "act", "dve", "pool", "sp"):
            self.engs[n] = {"sem": sems[n], "cnt": 0, "waited": {}, "ops": []}
        self.dma_sems = dma_sems
        self.dma_state = [0] * len(dma_sems)
        self.dma_rr = 0
        self.n_sw = 4
        self.dma_rr_sw = 0
        self.semobj = {}
        for n in sems:
            self.semobj[id(sems[n])] = sems[n]
        for s in dma_sems:
            self.semobj[id(s)] = s

    def _deps(self, eng, reads, writes):
        deps = {}

        def add(tok):
            if tok is None:
                return
            s, v = tok
            if deps.get(s, 0) < v:
                deps[s] = v
        for b in reads:
            add(b.w)
        for b in writes:
            add(b.w)
            for s, v in b.r.items():
                add((s, v))
        e = self.engs[eng]
        waits = []
        for s, v in deps.items():
            if eng == "pe" and s == id(e["sem"]):
                continue
            if e["waited"].get(s, 0) < v:
                e["waited"][s] = v
                waits.append((self.semobj[s], v))
        return waits

    def _commit(self, tok, reads, writes):
        s, v = tok
        for b in writes:
            b.w = tok
            b.r = {}
        for b in reads:
            if b.r.get(s, 0) < v:
                b.r[s] = v

    def op(self, eng, fn, reads=(), writes=()):
        e = self.engs[eng]
        waits = self._deps(eng, reads, writes)
        e["cnt"] += 1
        tok = (id(e["sem"]), e["cnt"])
        e["ops"].append((waits, fn, e["sem"], 1))
        self._commit(tok, reads, writes)

    def dma(self, eng, fn, reads=(), writes=()):
        e = self.engs[eng]
        if eng == "pool":
            i = self.dma_rr_sw
            self.dma_rr_sw = (self.dma_rr_sw + 1) % self.n_sw
        else:
            i = self.n_sw + self.dma_rr
            self.dma_rr = (self.dma_rr + 1) % (len(self.dma_sems) - self.n_sw)
        sem = self.dma_sems[i]
        waits = self._deps(eng, reads, writes)
        prev = self.dma_state[i]
        if prev > 0 and e["waited"].get(id(sem), 0) < prev:
            e["waited"][id(sem)] = prev
            waits.append((sem, prev))
        self.dma_state[i] = prev + 16
        tok = (id(sem), prev + 16)
        e["ops"].append((waits, fn, sem, 16))
        self._commit(tok, reads, writes)

    def barrier(self):
        toks = []
        for n, e in self.engs.items():
            if e["cnt"] > 0:
                toks.append((e["sem"], e["cnt"]))
        for i, s in enumerate(self.dma_sems):
            if self.dma_state[i] > 0:
                toks.append((s, self.dma_state[i]))
        for n, e in self.engs.items():
            waits = []
            for s, v in toks:
                if s is e["sem"]:
                    continue
                if e["waited"].get(id(s), 0) < v:
                    e["waited"][id(s)] = v
                    waits.append((s, v))
            if waits:
                e["ops"].append((waits, None, None, 0))

    def emit(self, eng, h):
        for waits, fn, sem, inc in self.engs[eng]["ops"]:
            for s, v in waits:
                h.wait_ge(s, v)
            if fn is not None:
                fn(h).then_inc(sem, inc)


def build(NM, debug=False):
    STOP = int(os.environ.get('KSTOP', '2'))
    NOW = int(os.environ.get('KNOW', '1'))
    CUT = int(os.environ.get('KCUT', '9'))
    PXN = int(os.environ.get('PXN', '2')); PHT = int(os.environ.get('PHT', '2')); PQT = int(os.environ.get('PQT', '2'))
    NL = 2 * NM
    HALO = NM - 1
    KV0 = max(0, HALO - 16)
    NQ = NM + 1
    nc = bass.Bass("TRN2", target_bir_lowering=False)

    def din(name, shape, dt=F32):
        return nc.dram_tensor(name, list(shape), dt, kind="ExternalInput").ap()

    xl = din("xl", [NL * 128, D])
    cT = din("cT", [128, 8])
    w_ada = din("w_ada", [D, 6 * D])
    b_ada = din("b_ada", [1, 6 * D])
    g1row = din("g1row", [1, D])
    g2row = din("g2row", [1, D])
    w_in = din("w_in", [D, IN_W])
    wg = din("wg", [16, 256])
    nbgT = din("nbgT", [128, 2])
    ggb = din("ggb", [128, 512])
    gqc = din("gqc", [128, 1])
    gkc = din("gkc", [128, 1])
    w_out = din("w_out", [D, D])
    w_up = din("w_up", [D, 2 * DFF])
    cwT = din("cwT", [128, 3, NFC])
    cbT = din("cbT", [128, NFC])
    w_down = din("w_down", [DFF, D])
    flag = din("flag", [128, 1])
    ident = din("ident", [128, 128], BF16)
    nsI = din("nsI", [128, 8, 128], BF16)
    dn = din("dn", [128, 24, 128], BF16)
    tri4 = din("tri4", [128, 4, 128])
    out = nc.dram_tensor("out", [NM * 128, D], F32, kind="ExternalOutput").ap()
    x1d = nc.dram_tensor("x1d", [NQ * 128, D], F32,
                         kind="ExternalOutput" if debug else "Internal").ap()

    es = ExitStack()
    with es:
        def sb(name, shape, dt=F32):
            return es.enter_context(nc.sbuf_tensor(name, list(shape), dt))

        def ps(name, shape, dt=F32):
            return es.enter_context(nc.psum_tensor(name, list(shape), dt))

        sems = {n: es.enter_context(nc.semaphore("s_" + n)) for n in ("pe", "act", "dve", "pool", "sp")}
        dma_sems = [es.enter_context(nc.semaphore("s_dma%d" % i)) for i in range(24)]
        K = Sched(nc, sems, dma_sems)

        pT = ps("pT", [128, 1024], BF16); bT = Buf("T")
        pP = [ps("pP%d" % i, [128, 512]) for i in range(2)]; bP = [Buf("P0"), Buf("P1")]
        pF = ps("pF", [128, 4, 128]); bF = Buf("F")
        pG = ps("pG", [128, 512]); bG = Buf("G")
        pS = [ps("pS%d" % i, [128, 512]) for i in range(2)]; bS = [Buf("S0"), Buf("S1")]
        pO = ps("pO", [128, 4, 128]); bO = Buf("O")

        ident_sb = sb("ident_sb", [128, 128], BF16); b_ident = Buf()
        flag_sb = sb("flag_sb", [128, 1]); b_flag = Buf()
        modT = sb("modT", [128, 48]); b_modT = Buf()
        G1 = sb("G1c", [128, 8]); G2 = sb("G2c", [128, 8]); b_G = Buf()
        g1b = sb("g1b", [128, D]); g2b = sb("g2b", [128, D]); b_gb = Buf()

        def dma_in(eng, dst, src, wbuf):
            K.dma(eng, lambda h: h.dma_start(out=dst, in_=src), writes=[wbuf])

        epsc = sb("epsc", [128, 1])
        b_eps = Buf()
        K.op("pool", lambda h: h.memset(epsc[:], EPS), writes=[b_eps])
        dma_in("sp", ident_sb[:], ident, b_ident)
        dma_in("sp", flag_sb[:], flag, b_flag)

        with ExitStack() as e0:
            def sb0(name, shape, dt=F32):
                return e0.enter_context(nc.sbuf_tensor(name, list(shape), dt))
            c_sb = sb0("c_sb", [128, 8]); b_c = Buf()
            cond = sb0("cond", [128, 8]); b_cond = Buf()
            wa = [sb0("wa%d" % i, [128, 8, 512]) for i in range(3)]; b_wa = [Buf() for _ in range(3)]
            brow = sb0("brow", [1, 6 * D]); b_brow = Buf()
            mrow = sb0("mrow", [1, 6 * D]); b_mrow = Buf()
            grow = sb0("grow", [1, 2 * D]); b_grow = Buf()
            ones1 = sb0("ones1", [1, 128]); b_ones = Buf()
            dma_in("sp", c_sb[:], cT, b_c)
            dma_in("sp", brow[:], b_ada, b_brow)
            dma_in("sp", grow[:, 0:D], g1row, b_grow)
            dma_in("sp", grow[:, D:2 * D], g2row, b_grow)
            K.op("dve", lambda h: h.memset(ones1[:], 1.0), writes=[b_ones])
            K.op("act", lambda h: h.activation(out=cond[:], in_=c_sb[:], func=AF.Silu),
                 reads=[b_c], writes=[b_cond])
            w_ada_v = w_ada.rearrange("(p k) n -> p k n", k=8)
            for blk in range(12):
                s = blk % 3
                dma_in("sp", wa[s][:], w_ada_v[:, :, blk * 512:(blk + 1) * 512], b_wa[s])
                pb = pP[blk % 2]; bb = bP[blk % 2]
                for k in range(8):
                    K.op("pe", lambda h, pb=pb, s=s, k=k: h.matmul(
                        pb[0:1, :], lhsT=cond[:, k:k + 1], rhs=wa[s][:, k, :],
                        start=(k == 0), stop=(k == 7)),
                        reads=[b_cond, b_wa[s]], writes=[bb])
                K.op("dve", lambda h, pb=pb, blk=blk: h.tensor_tensor(
                    out=mrow[:, blk * 512:(blk + 1) * 512], in0=pb[0:1, :],
                    in1=brow[:, blk * 512:(blk + 1) * 512], op=ALU.add),
                    reads=[bb, b_brow], writes=[b_mrow])
            for j in range(48):
                K.op("pe", lambda h, j=j: h.matmul(
                    pG[:, j:j + 1], lhsT=mrow[0:1, j * 128:(j + 1) * 128], rhs=ones1[0:1, 0:1],
                    start=True, stop=True), reads=[b_mrow, b_ones], writes=[bG])
            K.op("dve", lambda h: h.tensor_copy(out=modT[:], in_=pG[:, 0:48]), reads=[bG], writes=[b_modT])
            for gi, (gb, col0) in enumerate(((g1b, 2 * D), (g2b, 5 * D))):
                for hf in range(2):
                    pb = pP[hf]; bb = bP[hf]
                    K.op("pe", lambda h, pb=pb, col0=col0, hf=hf: h.matmul(
                        pb[:, :], lhsT=ones1[0:1, :], rhs=mrow[0:1, col0 + hf * 512: col0 + (hf + 1) * 512],
                        start=True, stop=True), reads=[b_mrow, b_ones], writes=[bb])
                    K.op("dve", lambda h, pb=pb, gb=gb, hf=hf: h.tensor_copy(
                        out=gb[:, hf * 512:(hf + 1) * 512], in_=pb[:, :]), reads=[bb], writes=[b_gb])
            n1c = sb0("n1c", [128, 16]); b_n1c = Buf()
            for j in range(16):
                K.op("pe", lambda h, j=j: h.matmul(
                    pG[:, 64 + j:65 + j], lhsT=grow[0:1, j * 128:(j + 1) * 128], rhs=ones1[0:1, 0:1],
                    start=True, stop=True), reads=[b_grow, b_ones], writes=[bG])
            K.op("dve", lambda h: h.tensor_copy(out=n1c[:], in_=pG[:, 64:80]), reads=[bG], writes=[b_n1c])
            K.op("dve", lambda h: h.scalar_tensor_tensor(
                out=G1[:], in0=modT[:, 8:16], scalar=1.0, in1=n1c[:, 0:8], op0=ALU.add, op1=ALU.mult),
                reads=[b_modT, b_n1c], writes=[b_G])
            K.op("dve", lambda h: h.scalar_tensor_tensor(
                out=G2[:], in0=modT[:, 32:40], scalar=1.0, in1=n1c[:, 8:16], op0=ALU.add, op1=ALU.mult),
                reads=[b_modT, b_n1c], writes=[b_G])
            K.barrier()
        sh1 = modT[:, 0:8]
        sh2 = modT[:, 24:32]

        with ExitStack() as e1:
            def sb1(name, shape, dt=F32):
                return e1.enter_context(nc.sbuf_tensor(name, list(shape), dt))
            win = sb1("win", [128, 8, IN_W], BF16); b_wins = [Buf() for _ in range(4)]
            wout = sb1("wout", [128, 8, D], BF16); b_wout = Buf()
            wg_sb = sb1("wg_sb", [16, 256], BF16); b_wg = Buf()
            nbg = sb1("nbg", [128, 2]); gg = sb1("gg", [128, 512])
            gq_sb = sb1("gq_sb", [128, 1]); gk_sb = sb1("gk_sb", [128, 1]); b_small = Buf()
            nsI_sb = sb1("nsI_sb", [128, 8, 128], BF16); dn_sb = sb1("dn_sb", [128, 24, 128], BF16)
            tri_sb = sb1("tri_sb", [128, 4, 128]); b_cst = Buf()
            stg = [sb1("stg%d" % i, [128, 8, 128]) for i in range(2)]; b_stg = [Buf(), Buf()]

            w_in_v = w_in.rearrange("(c p) n -> p c n", p=128)
            for i in range(4):
                c0 = i * 772
                K.dma("pool", lambda h, c0=c0: h.dma_start(out=win[:, :, c0:c0 + 772], in_=w_in_v[:, :, c0:c0 + 772]),
                      writes=[b_wins[i]])
            K.dma("pool", lambda h: h.dma_start(out=wg_sb[:], in_=wg), writes=[b_wg])
            dma_in("sp", nbg[:], nbgT, b_small); dma_in("sp", gg[:], ggb, b_small)
            dma_in("sp", gq_sb[:], gqc, b_small); dma_in("sp", gk_sb[:], gkc, b_small)
            dma_in("sp", nsI_sb[:], nsI, b_cst); dma_in("sp", dn_sb[:], dn, b_cst)
            dma_in("sp", tri_sb[:], tri4, b_cst)
            w_out_v = w_out.rearrange("(c p) n -> p c n", p=128)
            for i in range(8):
                s = i % 2
                dma_in("sp", stg[s][:], w_out_v[:, :, i * 128:(i + 1) * 128], b_stg[s])
                K.op("dve", lambda h, s=s, i=i: h.tensor_tensor(
                    out=wout[:, :, i * 128:(i + 1) * 128], in0=stg[s][:],
                    in1=g1b[:, i * 128:(i + 1) * 128].rearrange("p (o n) -> p o n", o=1).broadcast_to([128, 8, 128]),
                    op=ALU.mult), reads=[b_stg[s], b_gb], writes=[b_wout])

            xt = [sb1("xt%d" % i, [128, D]) for i in range(3)]; b_xt = [Buf() for _ in range(3)]
            junk = sb1("junk", [128, D], BF16); b_junk = Buf()
            sq = sb1("sq", [128, 512]); b_sq = Buf()
            ss = sb1("ss", [128, 4]); b_ss = Buf()
            xn = [sb1("xn%d" % i, [128, D], BF16) for i in range(2)]; b_xn = [Buf(), Buf()]
            hT = [sb1("hT%d" % i, [128, 8, 128], BF16) for i in range(2)]; b_hT = [Buf(), Buf()]; b_hTo = [Buf(), Buf()]
            QT = [sb1("QT%d" % i, [128, 2, 4, 128], BF16) for i in range(2)]; b_QT = [Buf(), Buf()]
            KT = [sb1("KT%d" % i, [128, 4, 128], BF16) for i in range(KVRING)]; b_KT = [Buf() for _ in range(KVRING)]
            VA = [sb1("VA%d" % i, [128, 8, 65], BF16) for i in range(KVRING)]; b_VA = [Buf() for _ in range(KVRING)]
            Pt = [sb1("Pt%d" % i, [128, 512], BF16) for i in range(3)]; b_Pt = [Buf() for _ in range(3)]
            qkn = sb1("qkn", [128, 512], BF16); b_qkn = Buf()
            ss8 = sb1("ss8", [128, 16]); b_ss8 = Buf()
            glrT = sb1("glrT", [16, 128], BF16); b_glrT = Buf()
            ge = sb1("ge", [128, 2, 128]); b_ge = Buf()
            gl = sb1("gl", [128, 2, 128]); b_gl = Buf()
            gbl = sb1("gbl", [128, 2, 128]); b_gbl = Buf()
            nbl = sb1("nbl", [128, 2]); b_nbl = Buf()
            Eb = sb1("Eb", [128, 2, 128]); Enb = sb1("Enb", [128, 2, 128]); Ee = sb1("Ee", [128, 2, 128])
            b_Eb = Buf(); b_Enb = Buf(); b_Ee = Buf()
            zeros = sb1("zeros", [128, 128]); b_zeros = Buf()
            QdT = sb1("QdT", [128, 2, 2, 128], BF16); KdT = sb1("KdT", [128, 2, 128], BF16)
            KeT = sb1("KeT", [128, 2, 128], BF16); b_QdT = Buf(); b_KdT = Buf(); b_KeT = Buf()
            Ke = sb1("Ke", [128, 256], BF16); b_Ke = Buf()
            vsb = sb1("vsb", [128, 512], BF16); b_vsb = Buf()
            sr = sb1("sr", [128, 512]); b_sr = Buf()
            At = sb1("At", [128, 4, 128], BF16); b_At = Buf()
            S = sb1("S", [128, 2, 128]); Sbf = sb1("Sbf", [128, 2, 128], BF16); b_S = Buf(); b_Sbf = Buf()
            ss4 = sb1("ss4", [128, 8]); b_ss4 = Buf()
            rden = sb1("rden", [128, 8]); b_rden = Buf()
            mixs = [sb1("mix%d" % i, [128, D], BF16) for i in range(2)]; bmixs = [Buf(), Buf()]
            mixT = sb1("mixT", [128, 8, 128], BF16); b_mixT = Buf()

            K.op("pool", lambda h: h.memset(zeros[:], 0.0), writes=[b_zeros])
            K.op("pool", lambda h: h.memset(QdT[:], 0.0), writes=[b_QdT])
            for i in range(2):
                K.op("pool", lambda h, i=i: h.memset(QT[i][:], 0.0), writes=[b_QT[i]])
            K.op("pool", lambda h: h.memset(S[:], 0.0), writes=[b_S])
            K.op("pool", lambda h: h.memset(Sbf[:], 0.0), writes=[b_Sbf])

            def proj_tok(j, col0, pi, extra=()):
                hb = hT[j % PHT]
                for c in range(8):
                    K.op("pe", lambda h, c=c, hb=hb: h.matmul(
                        pP[pi][:, :], lhsT=hb[:, c, :], rhs=win[:, c, col0:col0 + 512],
                        start=(c == 0), stop=(c == 7)), reads=[b_hT[j % PHT], b_hTo[j % PHT]] + b_wins + list(extra), writes=[bP[pi]])

            def proj_feat(j, col0, dst, wbuf, width=128):
                hb = hT[j % PHT]
                for c in range(8):
                    K.op("pe", lambda h, c=c, hb=hb: h.matmul(
                        dst, lhsT=win[:, c, col0:col0 + width], rhs=hb[:, c, :],
                        start=(c == 0), stop=(c == 7)), reads=[b_hT[j % PHT], b_hTo[j % PHT]] + b_wins, writes=[wbuf])

            loaded_x = set()

            def load_x(j):
                if j in loaded_x or j >= NL:
                    return
                loaded_x.add(j)
                dma_in("sp", xt[j % 3][:], xl[j * 128:(j + 1) * 128, :], b_xt[j % 3])

            def front(j):
                mixb, bmix = mixs[j % 2], bmixs[j % 2]
                pGo = pP[0][:, :].rearrange("p (a b) -> p a b", a=4)
                kind = "FULL" if j >= HALO else ("GK" if j >= KV0 else "G")
                xb, bxb = xt[j % 3], b_xt[j % 3]
                xnb, bxn = xn[j % PXN], b_xn[j % PXN]
                hb, bhb = hT[j % PHT], b_hT[j % PHT]
                sl = j % KVRING
                va, bva = VA[sl], b_VA[sl]
                qb = QT[j % PQT]
                load_x(j)
                if kind != "FULL":
                    load_x(j + 1)
                    load_x(j + 2)

                def o_norm():
                    K.op("act", lambda h: h.activation(out=junk[:], in_=xb[:], func=AF.Square, accum_out=ss[:, 0:1]),
                         reads=[bxb], writes=[b_junk, b_ss])
                    K.op("act", lambda h: h.activation(out=ss[:, 2:3], in_=ss[:, 0:1], func=AF.Ln, scale=1.0 / D, bias=epsc[:, 0:1]),
                         reads=[b_ss], writes=[b_ss])
                    K.op("act", lambda h: h.activation(out=ss[:, 1:2], in_=ss[:, 2:3], func=AF.Exp, scale=-0.5),
                         reads=[b_ss], writes=[b_ss])

                def o_xn():
                    K.op("dve", lambda h: h.tensor_scalar(out=xnb[:], in0=xb[:], scalar1=ss[:, 1:2], scalar2=None, op0=ALU.mult),
                         reads=[bxb, b_ss], writes=[bxn])

                def o_xT():
                    for c in range(8):
                        K.op("pe", lambda h, c=c: h.transpose(pT[:, c * 128:(c + 1) * 128], xnb[:, c * 128:(c + 1) * 128], ident_sb[:]),
                             reads=[bxn, b_ident], writes=[bT])

                def o_hT():
                    for c in range(8):
                        K.op("dve", lambda h, c=c: h.tensor_scalar(
                            out=hb[:, c, :], in0=pT[:, c * 128:(c + 1) * 128], scalar1=G1[:, c:c + 1], scalar2=sh1[:, c:c + 1],
                            op0=ALU.mult, op1=ALU.add), reads=[bT, b_G, b_modT], writes=[bhb])

                def o_glr():
                    proj_feat(j, C_GLR, pG[0:16, 0:128], bG, width=16)

                def o_gq():
                    for q in range(2):
                        proj_feat(j, C_GQ + q * 128, pF[:, q, :], bF)

                def o_gk():
                    for q in range(2):
                        proj_feat(j, C_GK + q * 128, pF[:, 2 + q, :], bF)

                def o_glrT():
                    K.op("dve", lambda h: h.tensor_copy(out=glrT[:], in_=pG[0:16, 0:128]), reads=[bG], writes=[b_glrT])

                def o_z():
                    for f in range(2):
                        K.op("pe", lambda h, f=f: h.matmul(pG[:, 128 + f * 128:256 + f * 128], lhsT=wg_sb[0:16, f * 128:(f + 1) * 128],
                                                           rhs=glrT[0:16, :], start=True, stop=True),
                             reads=[b_glrT, b_wg], writes=[bG])

                def o_ge():
                    for f in range(2):
                        K.op("act", lambda h, f=f: h.activation(out=ge[:, f, :], in_=pG[:, 128 + f * 128:256 + f * 128], func=AF.Exp,
                                                                scale=-1.0, bias=nbg[:, f:f + 1]),
                             reads=[bG, b_small], writes=[b_ge])
                    K.op("act", lambda h: h.activation(out=gl[:], in_=ge[:], func=AF.Ln, bias=1.0, scale=1.0),
                         reads=[b_ge], writes=[b_gl])

                def o_scan():
                    for f in range(2):
                        K.op("dve", lambda h, f=f: h.tensor_tensor_scan(out=gbl[:, f, :], data0=zeros[:], data1=gl[:, f, :],
                                                                        initial=0.0, op0=ALU.add, op1=ALU.add),
                             reads=[b_gl, b_zeros], writes=[b_gbl])
                    K.op("dve", lambda h: h.tensor_scalar(out=nbl[:], in0=gbl[:, :, 127], scalar1=-1.0 / 16.0, scalar2=None,
                                                          op0=ALU.mult), reads=[b_gbl], writes=[b_nbl])

                def o_EbEe():
                    K.op("act", lambda h: h.activation(out=Eb[:], in_=gbl[:], func=AF.Exp, scale=-1.0 / 16.0),
                         reads=[b_gbl], writes=[b_Eb])
                    for f in range(2):
                        K.op("act", lambda h, f=f: h.activation(out=Ee[:, f, :], in_=gbl[:, f, :], func=AF.Exp, scale=1.0 / 16.0,
                                                                bias=nbl[:, f:f + 1]), reads=[b_gbl, b_nbl], writes=[b_Ee])

                def o_Enb():
                    K.op("act", lambda h: h.activation(out=Enb[:], in_=gbl[:], func=AF.Exp, scale=1.0 / 16.0),
                         reads=[b_gbl], writes=[b_Enb])

                def o_KeT():
                    K.op("dve", lambda h: h.tensor_tensor(out=KeT[:], in0=pF[:, 2:4, :], in1=Ee[:], op=ALU.mult),
                         reads=[bF, b_Ee], writes=[b_KeT])

                def o_QdKd():
                    for pos in range(2):
                        p0 = pos * 64
                        K.op("dve", lambda h, pos=pos, p0=p0: h.scalar_tensor_tensor(
                            out=QdT[p0:p0 + 64, pos, :, :], in0=pF[p0:p0 + 64, 0:2, :], scalar=0.125, in1=Eb[p0:p0 + 64, :, :],
                            op0=ALU.mult, op1=ALU.mult), reads=[bF, b_Eb], writes=[b_QdT])
                    K.op("dve", lambda h: h.tensor_tensor(out=KdT[:], in0=pF[:, 2:4, :], in1=Enb[:], op=ALU.mult),
                         reads=[bF, b_Enb], writes=[b_KdT])

                def o_KeTt():
                    for f in range(2):
                        K.op("pe", lambda h, f=f: h.transpose(pT[:, f * 128:(f + 1) * 128], KeT[:, f, :], ident_sb[:]),
                             reads=[b_KeT, b_ident], writes=[bT])

                def o_Ke():
                    K.op("dve", lambda h: h.tensor_copy(out=Ke[:], in_=pT[:, 0:256]), reads=[bT], writes=[b_Ke])

                def o_v():
                    proj_tok(j, C_GV, 0)

                def o_vsb():
                    K.op("dve", lambda h: h.tensor_copy(out=vsb[:], in_=pP[0][:, :]), reads=[bP[0]], writes=[b_vsb])

                def o_r():
                    proj_tok(j, C_GR, 1)

                def o_rexp():
                    K.op("act", lambda h: h.activation(out=sr[:], in_=pP[1][:, :], func=AF.Exp, scale=-1.0), reads=[bP[1]], writes=[b_sr])

                def o_rln():
                    K.op("act", lambda h: h.activation(out=sr[:], in_=sr[:], func=AF.Ln, bias=1.0, scale=1.0), reads=[b_sr], writes=[b_sr])
                    K.op("act", lambda h: h.activation(out=sr[:], in_=sr[:], func=AF.Exp, scale=-1.0), reads=[b_sr], writes=[b_sr])

                def o_silu():
                    K.op("dve", lambda h: h.tensor_tensor(out=sr[:], in0=pP[1][:, :], in1=sr[:], op=ALU.mult),
                         reads=[bP[1], b_sr], writes=[b_sr])
                    K.op("pool", lambda h: h.tensor_tensor(out=sr[:], in0=sr[:], in1=gg[:], op=ALU.mult),
                         reads=[b_sr, b_small], writes=[b_sr])

                def o_AT():
                    for hh in range(4):
                        ch = hh // 2
                        K.op("pe", lambda h, hh=hh, ch=ch: h.matmul(
                            pF[:, hh, :], lhsT=KdT[:, ch, :], rhs=QdT[:, hh % 2, ch, :], start=True, stop=True),
                            reads=[b_KdT, b_QdT, b_KeT], writes=[bF])

                def o_At():
                    K.op("dve", lambda h: h.tensor_tensor(out=At[:], in0=pF[:], in1=tri_sb[:], op=ALU.mult),
                         reads=[bF, b_cst], writes=[b_At])

                def o_o():
                    for hh in range(4):
                        ch = hh // 2
                        K.op("pe", lambda h, hh=hh, ch=ch: h.matmul(
                            pGo[:, hh, :], lhsT=QdT[:, hh % 2, ch, :], rhs=Sbf[:, ch, :], start=True, stop=False),
                            reads=[b_QdT, b_Sbf], writes=[bP[0]])
                        K.op("pe", lambda h, hh=hh: h.matmul(
                            pGo[:, hh, :], lhsT=At[:, hh, :], rhs=vsb[:, hh * 128:(hh + 1) * 128], start=False, stop=True),
                            reads=[b_At, b_vsb], writes=[bP[0]])

                def o_osq():
                    K.op("act", lambda h: h.activation(out=sq[:], in_=pP[0][:, :], func=AF.Square),
                         reads=[bP[0]], writes=[b_sq])

                def o_ored():
                    K.op("dve", lambda h: h.tensor_reduce(out=ss4[:, 0:4], in_=sq[:].rearrange("p (a b) -> p a b", a=4),
                                                          axis=AX.X, op=ALU.add), reads=[b_sq], writes=[b_ss4])
                    K.op("act", lambda h: h.activation(out=ss4[:, 0:4], in_=ss4[:, 0:4], func=AF.Ln, scale=1.0 / 128, bias=epsc[:, 0:1]),
                         reads=[b_ss4], writes=[b_ss4])
                    K.op("act", lambda h: h.activation(out=ss4[:, 4:8], in_=ss4[:, 0:4], func=AF.Exp, scale=-0.5),
                         reads=[b_ss4], writes=[b_ss4])

                def o_mix():
                    for hh in range(4):
                        K.op("dve", lambda h, hh=hh: h.scalar_tensor_tensor(
                            out=mixb[:, hh * 128:(hh + 1) * 128], in0=pGo[:, hh, :], scalar=ss4[:, 4 + hh:5 + hh],
                            in1=sr[:, hh * 128:(hh + 1) * 128], op0=ALU.mult, op1=ALU.mult),
                            reads=[bP[0], b_ss4, b_sr], writes=[bmix])

                def o_st():
                    for hh in range(4):
                        ch = hh // 2
                        K.op("pe", lambda h, hh=hh, ch=ch: h.matmul(
                            pG[:, hh * 128:(hh + 1) * 128], lhsT=Ke[:, ch * 128:(ch + 1) * 128],
                            rhs=vsb[:, hh * 128:(hh + 1) * 128], start=True, stop=True),
                            reads=[b_Ke, b_vsb], writes=[bG])

                def o_S():
                    for hh in range(4):
                        pr = (hh % 2) * 64; ch = hh // 2
                        K.op("dve", lambda h, hh=hh, pr=pr, ch=ch: h.scalar_tensor_tensor(
                            out=S[pr:pr + 64, ch, :], in0=S[pr:pr + 64, ch, :], scalar=Eb[pr:pr + 64, ch, 127:128],
                            in1=pG[pr:pr + 64, hh * 128:(hh + 1) * 128],
                            op0=ALU.mult, op1=ALU.add), reads=[b_S, b_Eb, bG], writes=[b_S])
                    if j == HALO:
                        K.op("dve", lambda h: h.tensor_scalar(out=S[:], in0=S[:], scalar1=flag_sb[:, 0:1], scalar2=None, op0=ALU.mult),
                             reads=[b_S, b_flag], writes=[b_S])
                    K.op("pool", lambda h: h.tensor_copy(out=Sbf[:], in_=S[:]), reads=[b_S], writes=[b_Sbf])

                def mk_qk(nm, col0, o8):
                    def o_p():
                        proj_tok(j, col0, 1)

                    def o_sq():
                        K.op("act", lambda h: h.activation(out=sq[:], in_=pP[1][:, :], func=AF.Square), reads=[bP[1]], writes=[b_sq])

                    def o_red():
                        K.op("dve", lambda h: h.tensor_reduce(out=ss8[:, o8:o8 + 8], in_=sq[:].rearrange("p (a b) -> p a b", a=8),
                                                              axis=AX.X, op=ALU.add), reads=[b_sq], writes=[b_ss8])
                        K.op("act", lambda h: h.activation(out=ss8[:, o8:o8 + 8], in_=ss8[:, o8:o8 + 8], func=AF.Ln, scale=1.0 / 64,
                                                           bias=epsc[:, 0:1]), reads=[b_ss8], writes=[b_ss8])
                        K.op("act", lambda h: h.activation(out=ss8[:, o8:o8 + 8], in_=ss8[:, o8:o8 + 8], func=AF.Exp, scale=-0.5),
                             reads=[b_ss8], writes=[b_ss8])

                    def o_n():
                        K.op("dve", lambda h: h.tensor_tensor(
                            out=qkn[:].rearrange("p (a b) -> p a b", a=8), in0=pP[1][:, :].rearrange("p (a b) -> p a b", a=8),
                            in1=ss8[:, o8:o8 + 8].rearrange("p (a o) -> p a o", o=1).broadcast_to([128, 8, 64]), op=ALU.mult),
                            reads=[bP[1], b_ss8], writes=[b_qkn])

                    def o_T():
                        for c in range(4):
                            K.op("pe", lambda h, c=c: h.transpose(pT[:, c * 128:(c + 1) * 128], qkn[:, c * 128:(c + 1) * 128], ident_sb[:]),
                                 reads=[b_qkn, b_ident], writes=[bT])

                    def o_ev():
                        if nm == "k":
                            K.op("dve", lambda h: h.tensor_scalar(out=KT[sl][:].rearrange("p a b -> p (a b)"), in0=pT[:, 0:512],
                                                                  scalar1=gk_sb[:, 0:1], scalar2=None, op0=ALU.mult),
                                 reads=[bT, b_small], writes=[b_KT[sl]])
                        else:
                            for pos in range(2):
                                p0 = pos * 64
                                K.op("dve", lambda h, pos=pos, p0=p0: h.tensor_scalar(
                                    out=qb[p0:p0 + 64, pos, :, :], in0=pT[p0:p0 + 64, 0:512].rearrange("p (a b) -> p a b", a=4),
                                    scalar1=gq_sb[p0:p0 + 64, 0:1], scalar2=0.125, op0=ALU.mult, op1=ALU.mult),
                                    reads=[bT, b_small], writes=[b_QT[j % PQT]])
                    return o_p, o_sq, o_red, o_n, o_T, o_ev

                o_k, o_ksq, o_kred, o_kn, o_kT, o_KT = mk_qk("k", C_AK, 0)
                o_q, o_qsq, o_qred, o_qn, o_qT, o_QT = mk_qk("q", C_AQ, 8)

                def o_av():
                    proj_tok(j, C_AV, 0)

                def o_VA():
                    K.op("dve", lambda h: h.tensor_copy(out=va[:, :, 0:64], in_=pP[0][:, :].rearrange("p (a b) -> p a b", a=8)),
                         reads=[bP[0]], writes=[bva])
                    K.op("pool", lambda h: h.memset(va[:, :, 64:65], 1.0), writes=[bva])
                    if j < NM:
                        K.op("dve", lambda h: h.tensor_scalar(out=va[:], in0=va[:], scalar1=flag_sb[:, 0:1], scalar2=None,
                                                              op0=ALU.mult), reads=[bva, b_flag], writes=[bva])

                def o_warm():
                    if kind == "FULL":
                        return
                    for _ in range(5):
                        K.op("pe", lambda h: h.matmul(pS[0][:, :], lhsT=ident_sb[:], rhs=dn_sb[:, 0:4, :].rearrange("p a b -> p (a b)"),
                                                      start=True, stop=True), reads=[b_ident, b_cst], writes=[bS[0]])

                G_OPS = {o_warm, o_norm, o_xn, o_xT, o_hT, o_glr, o_gk, o_v, o_glrT, o_vsb, o_z, o_ge, o_scan, o_EbEe, o_KeT, o_KeTt, o_Ke, o_st, o_S}
                GK_OPS = G_OPS | {o_k, o_ksq, o_kred, o_kn, o_kT, o_KT, o_av, o_VA}
                sched = [
                    [o_norm], [], [o_xn], [], [o_xT], [], [o_hT], "A", [],
                    [o_glr, o_gk, o_gq], [o_v], [o_glrT, o_vsb], [o_r, o_z], [o_ge, o_rexp, o_warm], [o_scan, o_rln, o_warm],
                    [o_EbEe, o_Enb, o_warm], [o_KeT, o_QdKd, o_silu, o_warm], [o_KeTt, o_k], [o_Ke, o_AT, o_ksq, o_warm], [o_At, o_kred],
                    [o_o, o_st], [o_osq, o_kn, o_S], [o_ored, o_kT], [o_q, o_KT], [o_mix, o_qsq], [o_av, o_qred],
                    [o_VA, o_qn], [o_qT], [o_QT],
                ]
                allowed = None if kind == "FULL" else (GK_OPS if kind == "GK" else G_OPS)
                for step in sched:
                    if step == "A":
                        yield "A"
                        continue
                    did = False
                    for f in step:
                        if allowed is None or f in allowed:
                            f(); did = True
                    if did or kind == "FULL":
                        yield None

            def attn(j, filler):
                mixb, bmix = mixs[j % 2], bmixs[j % 2]
                xb, bxb = xt[j % 3], b_xt[j % 3]
                qb, bqb = QT[j % PQT], b_QT[j % PQT]
                groups = []
                for hh in range(8):
                    ents = [(i, g, m) for i, (g, m) in enumerate(ENTRIES) if j - m >= KV0]
                    k0 = 0
                    while k0 < len(ents):
                        grp = ents[k0:k0 + 4]
                        groups.append((hh, grp, k0 == 0, k0 + 4 >= len(ents)))
                        k0 += 4
                state = {"n": 0}

                def issue_qk(gi):
                    hh, grp, first, last = groups[gi]
                    pr = (hh % 2) * 64; ch = hh // 2
                    sbk = gi % 2
                    contiguous = all(grp[t][0] == grp[0][0] + t for t in range(len(grp)))
                    n = len(grp)
                    if contiguous:
                        i0 = grp[0][0]
                        K.op("pe", lambda h, hh=hh, i0=i0, n=n, sbk=sbk: h.matmul(
                            pS[sbk][:, 0:n * 128], lhsT=nsI_sb[:, hh, :], rhs=dn_sb[:, i0:i0 + n, :].rearrange("p a b -> p (a b)"),
                            start=True, stop=False), reads=[b_cst], writes=[bS[sbk]])
                    else:
                        for t, (i, g, m) in enumerate(grp):
                            K.op("pe", lambda h, hh=hh, i=i, t=t, sbk=sbk: h.matmul(
                                pS[sbk][:, t * 128:(t + 1) * 128], lhsT=nsI_sb[:, hh, :], rhs=dn_sb[:, i, :],
                                start=(t == 0), stop=False), reads=[b_cst], writes=[bS[sbk]])
                    for t, (i, g, m) in enumerate(grp):
                        ks = (j - m) % KVRING
                        K.op("pe", lambda h, t=t, ks=ks, hh=hh, ch=ch, sbk=sbk, n=n, qb=qb: h.matmul(
                            pS[sbk][:, t * 128:(t + 1) * 128], lhsT=KT[ks][:, ch, :], rhs=qb[:, hh % 2, ch, :],
                            start=False, stop=(t == n - 1)), reads=[b_KT[ks], bqb], writes=[bS[sbk]])
                    pb = gi % 3
                    K.op("act", lambda h, sbk=sbk, pb=pb, n=n: h.activation(out=Pt[pb][:, 0:n * 128], in_=pS[sbk][:, 0:n * 128], func=AF.Exp),
                         reads=[bS[sbk]], writes=[b_Pt[pb]])

                def issue_pv(gi):
                    hh, grp, first, last = groups[gi]
                    pb = gi % 3
                    n = len(grp)
                    for t, (i, g, m) in enumerate(grp):
                        ks = (j - m) % KVRING
                        K.op("pe", lambda h, hh=hh, t=t, ks=ks, pb=pb, st=(first and t == 0), sp=(last and t == n - 1): h.matmul(
                            pO[:, hh % 4, 0:65], lhsT=Pt[pb][:, t * 128:(t + 1) * 128], rhs=VA[ks][:, hh, :],
                            start=st, stop=sp), reads=[b_Pt[pb], b_VA[ks]], writes=[bO])

                def finish_half(half):
                    K.op("dve", lambda h, half=half: h.tensor_scalar(out=rden[:, half * 4:half * 4 + 4], in0=pO[:, :, 64], scalar1=1e-30,
                                                                     scalar2=None, op0=ALU.add), reads=[bO], writes=[b_rden])
                    K.op("dve", lambda h, half=half: h.reciprocal(out=rden[:, half * 4:half * 4 + 4], in_=rden[:, half * 4:half * 4 + 4]),
                         reads=[b_rden], writes=[b_rden])
                    for t in range(4):
                        hh = half * 4 + t
                        K.op("dve", lambda h, hh=hh, t=t: h.tensor_scalar(
                            out=mixb[:, 512 + hh * 64:512 + (hh + 1) * 64], in0=pO[:, t, 0:64], scalar1=rden[:, hh:hh + 1],
                            scalar2=None, op0=ALU.mult), reads=[bO, b_rden], writes=[bmix])

                ng = len(groups)
                LAG = 2
                for gi in range(ng + LAG):
                    if gi < ng:
                        issue_qk(gi)
                    if gi >= LAG:
                        issue_pv(gi - LAG)
                        hh_prev, _, _, last_prev = groups[gi - LAG]
                        if last_prev and hh_prev % 4 == 3:
                            finish_half(hh_prev // 4)
                    if filler is not None and gi >= 1:
                        next(filler, None)

                if filler is not None:
                    for _ in filler:
                        pass

            def tail(j):
                mixb, bmix = mixs[j % 2], bmixs[j % 2]
                xb, bxb = xt[j % 3], b_xt[j % 3]
                for c in range(8):
                    K.op("pe", lambda h, c=c: h.transpose(pT[:, c * 128:(c + 1) * 128], mixb[:, c * 128:(c + 1) * 128], ident_sb[:]),
                         reads=[bmix, b_ident], writes=[bT])
                yield
                yield
                K.op("dve", lambda h: h.tensor_copy(out=mixT[:].rearrange("p a b -> p (a b)"), in_=pT[:, :]), reads=[bT], writes=[b_mixT])
                yield
                yield
                x1b, bx1 = xb, bxb
                for hf in range(2):
                    for c in range(8):
                        K.op("pe", lambda h, c=c, hf=hf: h.matmul(
                            pP[hf][:, :], lhsT=mixT[:, c, :], rhs=wout[:, c, hf * 512:(hf + 1) * 512],
                            start=(c == 0), stop=(c == 7)), reads=[b_mixT, b_wout], writes=[bP[hf]])
                    K.op("dve", lambda h, hf=hf, x1b=x1b, xb=xb: h.tensor_tensor(
                        out=x1b[:, hf * 512:(hf + 1) * 512], in0=pP[hf][:, :], in1=xb[:, hf * 512:(hf + 1) * 512], op=ALU.add),
                        reads=[bP[hf], bxb], writes=[bx1])
                    yield
                qi = j - HALO
                K.dma("sp", lambda h, x1b=x1b, qi=qi: h.dma_start(out=x1d[qi * 128:(qi + 1) * 128, :], in_=x1b[:]),
                      reads=[bx1], writes=[])
                load_x(j + 3)

            def run_all(gen):
                if gen is not None:
                    for _ in gen:
                        pass

            import itertools
            pending = None
            pending_tail = None
            pre = [j for j in range(NL if STOP >= 1 else 0) if j < HALO]
            def adv_A(gen):
                for v in gen:
                    if v == "A":
                        break

            cur = None
            if pre:
                cur = front(pre[0]); adv_A(cur)
            for idx, j in enumerate(pre):
                nxt = None
                if idx + 1 < len(pre):
                    nxt = front(pre[idx + 1]); adv_A(nxt)
                run_all(cur)
                cur = nxt
            for j in range(NL if STOP >= 1 else 0):
                kind_j = "FULL" if j >= HALO else ("GK" if j >= KV0 else "G")
                if kind_j != "FULL":
                    continue
                if j == HALO:
                    run_all(front(j))
                    load_x(j + 1)
                    load_x(j + 2)
                else:
                    run_all(pending)
                pending = front(j + 1) if j + 1 < NL else None
                attn(j, itertools.chain(*[g for g in (pending_tail, pending) if g is not None]))
                pending_tail = tail(j)
            run_all(pending_tail)
            K.barrier()

        with ExitStack() as e2:
            def sb2(name, shape, dt=F32):
                return e2.enter_context(nc.sbuf_tensor(name, list(shape), dt))
            GT = 384
            NT = GT // 128
            wup = sb2("wup", [128, 8, 2 * DFF], BF16); b_wups = [Buf() for _ in range(8)]
            wdn = sb2("wdn", [128, NGC, D], BF16); b_wdn = Buf()
            cw = sb2("cw", [128, 3, NFC]); cb = sb2("cb", [128, NFC]); b_cw = Buf()
            dma_in("sp", cw[:], cwT, b_cw); dma_in("sp", cb[:], cbT, b_cw)
            w_up_v = w_up.rearrange("(c p) n -> p c n", p=128)
            for i in (0, 4, 1, 5, 2, 6, 3, 7):
                c0 = i * 704
                K.dma("pool", lambda h, c0=c0: h.dma_start(out=wup[:, :, c0:c0 + 704], in_=w_up_v[:, :, c0:c0 + 704]),
                      writes=[b_wups[i]])
            pU = [pP[0], pP[1], pS[0], pS[1]]; bU = [bP[0], bP[1], bS[0], bS[1]]
            pD = [pF, pO]; bD = [bF, bO]
            x1a = [sb2("x1a%d" % i, [128, D]) for i in range(3)]; b_x1a = [Buf(), Buf(), Buf()]
            x1r = [sb2("x1r%d" % i, [128, D]) for i in range(2)]; b_x1r = [Buf(), Buf()]
            stg2 = x1r; b_stg2 = b_x1r

            def wdn_loader():
                for i in range(NGC):
                    s_ = i % 2
                    dma_in("sp", stg2[s_][:], w_down[i * 128:(i + 1) * 128, :], b_stg2[s_])
                    if i >= 1:
                        yield
                    K.op("dve", lambda h, s_=s_, i=i: h.tensor_tensor(out=wdn[:, i, :], in0=stg2[s_][:], in1=g2b[:], op=ALU.mult),
                         reads=[b_stg2[s_], b_gb], writes=[b_wdn])
                    yield
            wdn_gen = wdn_loader()
            ssb = sb2("ssb", [128, 12]); b_ssbs = [Buf(), Buf(), Buf()]
            xn2 = [sb2("xn2_%d" % i, [128, D], BF16) for i in range(3)]; b_xn2 = [Buf(), Buf(), Buf()]
            h2T = [sb2("h2T%d" % i, [128, 8, GT], BF16) for i in range(1)] * 2; b_h2T = [Buf()] * 2
            NU = 4
            usb = [sb2("usb%d" % i, [128, GT + 2]) for i in range(NU)]; b_usb = [Buf() for _ in range(NU)]; b_ush = [Buf() for _ in range(NU)]
            NY = 5
            ysb = [sb2("ysb%d" % i, [128, GT]) for i in range(NY)]; b_ysb = [Buf() for _ in range(NY)]
            sg = [sb2("sg%d" % i, [128, GT]) for i in range(2)]; b_sg = [Buf(), Buf()]
            mT = [sb2("mT%d" % i, [128, NGC, GT], BF16) for i in range(1)]; b_mT = [Buf()]
            hal = sb2("hal", [128, NFC, 2]); b_hal = [Buf() for _ in range(NFC)]
            K.op("pool", lambda h: h.memset(hal[:], 0.0), writes=b_hal)

            groups2 = [[0]] + [list(range(q0, min(q0 + NT, NM + 1))) for q0 in range(1, NM + 1, NT)]
            if STOP < 2:
                groups2 = []
            tcnt = {"n": 0}

            loaded_g = set()

            def load_x1(gidx):
                if gidx in loaded_g or gidx >= len(groups2):
                    return
                loaded_g.add(gidx)
                for qi in groups2[gidx]:
                    K.dma("sp", lambda h, qi=qi: h.dma_start(out=x1a[qi % 3][:], in_=x1d[qi * 128:(qi + 1) * 128, :]),
                          writes=[b_x1a[qi % 3]])

            def front2_a(gidx):
                tiles = groups2[gidx]
                load_x1(gidx)
                for t, qi in enumerate(tiles):
                    xa, bxa = x1a[qi % 3], b_x1a[qi % 3]
                    xnb, bxn = xn2[qi % 3], b_xn2[qi % 3]
                    o4 = (qi % 3) * 4
                    b_ssb = b_ssbs[qi % 3]
                    K.op("act", lambda h, xa=xa, xnb=xnb, o4=o4: h.activation(out=xnb[:], in_=xa[:], func=AF.Square, accum_out=ssb[:, o4:o4 + 1]),
                         reads=[bxa], writes=[bxn, b_ssb])
                    K.op("act", lambda h, o4=o4: h.activation(out=ssb[:, o4 + 2:o4 + 3], in_=ssb[:, o4:o4 + 1], func=AF.Ln, scale=1.0 / D, bias=epsc[:, 0:1]),
                         reads=[b_ssb], writes=[b_ssb])
                    K.op("act", lambda h, o4=o4: h.activation(out=ssb[:, o4 + 1:o4 + 2], in_=ssb[:, o4 + 2:o4 + 3], func=AF.Exp, scale=-0.5),
                         reads=[b_ssb], writes=[b_ssb])
                    K.op("dve", lambda h, xa=xa, xnb=xnb, o4=o4: h.tensor_scalar(out=xnb[:], in0=xa[:], scalar1=ssb[:, o4 + 1:o4 + 2], scalar2=None,
                                                                                 op0=ALU.mult), reads=[bxa, b_ssb], writes=[bxn])

            def front2_b(gidx, only_t=None):
                if gidx >= len(groups2):
                    return
                tiles = groups2[gidx]
                hb, bhb = h2T[gidx % 2], b_h2T[gidx % 2]
                for t, qi in enumerate(tiles):
                    if only_t is not None and t != only_t:
                        continue
                    xnb, bxn = xn2[qi % 3], b_xn2[qi % 3]
                    for c in range(8):
                        K.op("pe", lambda h, c=c, xnb=xnb: h.transpose(pT[:, c * 128:(c + 1) * 128], xnb[:, c * 128:(c + 1) * 128], ident_sb[:]),
                             reads=[bxn, b_ident], writes=[bT])
                    for c in range(8):
                        K.op("act", lambda h, c=c, hb=hb, t=t: h.activation(
                            out=hb[:, c, t * 128:(t + 1) * 128], in_=pT[:, c * 128:(c + 1) * 128], func=AF.Identity,
                            scale=G2[:, c:c + 1], bias=sh2[:, c:c + 1]), reads=[bT, b_G, b_modT], writes=[bhb])

            def upconv(gidx):
                tiles = groups2[gidx]
                is_halo = (gidx == 0)
                ntok = 128 * len(tiles)
                hb, bhb = h2T[gidx % 2], b_h2T[gidx % 2]
                mb, bmb = mT[0], b_mT[0]
                load_x1(gidx + 1)
                order = []
                for i in range(NGC):
                    order += [i, NGC + i]

                def stA(oi):
                    fc = order[oi]
                    ub, bub = pU[oi % 4], bU[oi % 4]
                    us, bus, bush = usb[oi % NU], b_usb[oi % NU], b_ush[oi % NU]
                    for c in range(8):
                        K.op("pe", lambda h, c=c, fc=fc, ub=ub: h.matmul(
                            ub[:, 0:ntok], lhsT=wup[:, c, fc * 128:(fc + 1) * 128], rhs=hb[:, c, 0:ntok],
                            start=(c == 0), stop=(c == 7)), reads=[bhb] + [b_wups[c_] for c_ in range((fc * 128) // 704, (fc * 128 + 127) // 704 + 1)], writes=[bub])
                    K.op("pool", lambda h, fc=fc, us=us: h.tensor_copy(out=us[:, 0:2], in_=hal[:, fc, :]),
                         reads=[b_hal[fc]], writes=[bush])
                    K.op("act", lambda h, ub=ub, us=us: h.copy(out=us[:, 2:2 + ntok], in_=ub[:, 0:ntok]),
                         reads=[bub], writes=[bus])
                    K.op("pool", lambda h, fc=fc, us=us: h.tensor_copy(out=hal[:, fc, :], in_=us[:, ntok:ntok + 2]),
                         reads=[bus], writes=[b_hal[fc]])

                def stB(oi):
                    fc = order[oi]
                    us, bus, bush = usb[oi % NU], b_usb[oi % NU], b_ush[oi % NU]
                    yb, byb = ysb[oi % NY], b_ysb[oi % NY]
                    K.op("act", lambda h, fc=fc, us=us, yb=yb: h.activation(
                        out=yb[:, 0:ntok], in_=us[:, 2:2 + ntok], func=AF.Identity, scale=cw[:, 2, fc:fc + 1], bias=cb[:, fc:fc + 1]),
                        reads=[bus, b_cw], writes=[byb])
                    K.op("dve", lambda h, fc=fc, us=us, yb=yb: h.scalar_tensor_tensor(
                        out=yb[:, 0:ntok], in0=us[:, 1:1 + ntok], scalar=cw[:, 1, fc:fc + 1], in1=yb[:, 0:ntok],
                        op0=ALU.mult, op1=ALU.add), reads=[bus, bush, b_cw, byb], writes=[byb])
                    K.op("dve", lambda h, fc=fc, us=us, yb=yb: h.scalar_tensor_tensor(
                        out=yb[:, 0:ntok], in0=us[:, 0:ntok], scalar=cw[:, 0, fc:fc + 1], in1=yb[:, 0:ntok],
                        op0=ALU.mult, op1=ALU.add), reads=[bus, bush, b_cw, byb], writes=[byb])

                def stC1(i):
                    yg, byg = ysb[(2 * i) % NY], b_ysb[(2 * i) % NY]
                    sgb, bsg = sg[i % 2], b_sg[i % 2]
                    K.op("act", lambda h, yg=yg, sgb=sgb: h.activation(out=sgb[:, 0:ntok], in_=yg[:, 0:ntok], func=AF.Silu),
                         reads=[byg], writes=[bsg])

                def stC2(i):
                    yv, byv = ysb[(2 * i + 1) % NY], b_ysb[(2 * i + 1) % NY]
                    sgb, bsg = sg[i % 2], b_sg[i % 2]
                    K.op("pool", lambda h, i=i, sgb=sgb, yv=yv: h.tensor_tensor(
                        out=mb[:, i, 0:ntok], in0=sgb[:, 0:ntok], in1=yv[:, 0:ntok], op=ALU.mult),
                        reads=[bsg, byv], writes=[bmb])

                n = len(order)
                for oi in range(n + 5):
                    if is_halo:
                        next(wdn_gen, None)
                    if oi == 20 and gidx + 1 < len(groups2):
                        front2_a(gidx + 1)
                    if oi >= n and (oi - n) % 2 == 0:
                        front2_b(gidx + 1, only_t=(oi - n) // 2)
                    if oi < n:
                        stA(oi)
                    if (not is_halo) and 0 <= oi - 1 < n:
                        stB(oi - 1)
                    if (not is_halo) and 0 <= oi - 3 < n and (oi - 3) % 2 == 0:
                        stC1((oi - 3) // 2)
                    if (not is_halo) and 0 <= oi - 5 < n and (oi - 5) % 2 == 0:
                        stC2((oi - 5) // 2)
                if is_halo:
                    K.op("dve", lambda h: h.tensor_scalar(out=hal[:], in0=hal[:], scalar1=flag_sb[:, 0:1], scalar2=None, op0=ALU.mult),
                         reads=b_hal + [b_flag], writes=b_hal)

            def down(gidx):
                tiles = groups2[gidx]
                mb, bmb = mT[0], b_mT[0]
                for t, qi in enumerate(tiles):
                    xr, bxr = x1r[qi % 2], b_x1r[qi % 2]
                    K.dma("sp", lambda h, xr=xr, qi=qi: h.dma_start(out=xr[:], in_=x1d[qi * 128:(qi + 1) * 128, :]), writes=[bxr])
                    for hf in range(2):
                        db, bdb = pD[hf], bD[hf]
                        dflat = db[:].rearrange("p a b -> p (a b)")
                        for i in range(NGC):
                            K.op("pe", lambda h, i=i, hf=hf, dflat=dflat, t=t: h.matmul(
                                dflat, lhsT=mb[:, i, t * 128:(t + 1) * 128], rhs=wdn[:, i, hf * 512:(hf + 1) * 512],
                                start=(i == 0), stop=(i == NGC - 1)), reads=[bmb, b_wdn], writes=[bdb])
                        K.op("dve", lambda h, hf=hf, dflat=dflat, xr=xr: h.tensor_tensor(
                            out=xr[:, hf * 512:(hf + 1) * 512], in0=dflat, in1=xr[:, hf * 512:(hf + 1) * 512], op=ALU.add),
                            reads=[bdb, bxr], writes=[bxr])
                    oi_ = qi - 1
                    K.dma("sp", lambda h, xr=xr, oi_=oi_: h.dma_start(out=out[oi_ * 128:(oi_ + 1) * 128, :], in_=xr[:]),
                          reads=[bxr], writes=[])

            if groups2:
                front2_a(0)
                front2_b(0)
            for gidx in range(len(groups2)):
                upconv(gidx)
                if gidx == 0:
                    for _ in wdn_gen:
                        pass
                if gidx > 0:
                    down(gidx)
            K.barrier()

        with nc.Block() as block:
            @block.tensor
            def _(e):
                K.emit("pe", e)

            @block.scalar
            def _(e):
                K.emit("act", e)

            @block.vector
            def _(e):
                K.emit("dve", e)

            @block.gpsimd
            def _(e):
                K.emit("pool", e)

            @block.sync
            def _(e):
                K.emit("sp", e)
    return nc


def _consts():
    bf = ml_dtypes.bfloat16
    ident = np.eye(128, dtype=np.float32).astype(bf)
    slopes = np.array([2.0 ** (-8.0 * (h + 1) / 8) for h in range(8)], dtype=np.float32)
    nsI = np.zeros((128, 8, 128), np.float32)
    for h in range(8):
        nsI[np.arange(128), h, np.arange(128)] = -slopes[h]
    k = np.arange(128)[:, None]
    q = np.arange(128)[None, :]
    dnm = np.zeros((128, 24, 128), np.float32)
    for i, (g, m) in enumerate(ENTRIES):
        d = DILS[g]
        dist = 128 * m + q - k
        valid = (dist >= 0) & (dist % d == 0) & (dist <= 128 * d)
        dnm[:, i, :] = np.where(valid, dist, BIGD)
    tri = (k <= q).astype(np.float32)
    tri4 = np.repeat(tri[:, None, :], 4, axis=1)
    return ident, nsI.astype(bf), dnm.astype(bf), np.ascontiguousarray(tri4)


def make_in_maps(inputs, NM):
    S_half = NM * 128
    x = np.asarray(inputs["x"], np.float32)
    B = x.shape[0]
    ident, nsI, dnm, tri4 = _consts()
    f = lambda a: np.ascontiguousarray(np.asarray(a, np.float32))
    shared = {
        "w_ada": f(inputs["w_ada"][0]), "b_ada": f(inputs["b_ada"][0]).reshape(1, -1),
        "g1row": f(inputs["norm1_g"][0]).reshape(1, -1), "g2row": f(inputs["norm2_g"][0]).reshape(1, -1),
        "w_in": f(inputs["w_in"][0]), "wg": f(inputs["gla_w_gate"][0]),
        "nbgT": f(-np.asarray(inputs["gla_b_gate"][0], np.float32).reshape(2, 128).T),
        "ggb": f(np.tile(np.asarray(inputs["gla_norm_g"][0], np.float32)[None, :], (128, 4))),
        "gqc": f(np.tile(np.asarray(inputs["q_norm_g"][0], np.float32), 2).reshape(128, 1)),
        "gkc": f(np.tile(np.asarray(inputs["k_norm_g"][0], np.float32), 2).reshape(128, 1)),
        "w_out": f(inputs["w_out"][0]), "w_up": f(inputs["w_up"][0]),
        "cwT": f(np.asarray(inputs["conv_w"][0], np.float32).reshape(3, NFC, 128).transpose(2, 0, 1)),
        "cbT": f(np.asarray(inputs["conv_b"][0], np.float32).reshape(NFC, 128).T),
        "w_down": f(inputs["w_down"][0]),
        "ident": ident, "nsI": nsI, "dn": dnm, "tri4": tri4,
    }
    maps = []
    for core in range(2 * B):
        b, half = core // 2, core % 2
        if half == 0:
            xloc = np.concatenate([np.zeros((S_half, D), np.float32), x[b, :S_half]], axis=0)
        else:
            xloc = x[b, :2 * S_half]
        m = dict(shared)
        m["xl"] = np.ascontiguousarray(xloc)
        m["cT"] = f(np.asarray(inputs["c"], np.float32)[b].reshape(128, 8))
        m["flag"] = np.full((128, 1), float(half), np.float32)
        maps.append(m)
    return maps


_NC_CACHE = {}


def run(inputs, NM, debug=False):
    key = (NM, debug)
    if key not in _NC_CACHE:
        _NC_CACHE[key] = build(NM, debug)
    nc = _NC_CACHE[key]
    maps = make_in_maps(inputs, NM)
    res = run_bass_kernel_spmd(nc, maps, core_ids=list(range(len(maps))))
    return res


def kernel(**inputs):
    NM = 32
    res = run(inputs, NM)
    B = np.asarray(inputs["x"]).shape[0]
    outp = np.zeros((B, 2 * NM * 128, D), np.float32)
    for core in range(2 * B):
        b, half = core // 2, core % 2
        outp[b, half * NM * 128:(half + 1) * NM * 128] = np.asarray(res.results[core]["out"], np.float32)
    return outp
```
